# Optimizing a Trainium2 kernel written in Bass

```python
import jax, jax.numpy as jnp
from jax import lax
import numpy as np

D_MODEL = 2048
BATCH = 4
SEQ = 4096
DEPTH = 4

N_A_LAYERS = DEPTH // 2
N_B_LAYERS = DEPTH - N_A_LAYERS
D_FF = 5632
GMLP_CHUNK = 128
GMLP_D_GATE = D_MODEL
GMLP_GROUP_WIDTH = 128
GMLP_GROUPS = GMLP_D_GATE // GMLP_GROUP_WIDTH
HEAD_DIM = 128
N_HEADS = D_MODEL // HEAD_DIM
DILATED_GROUPS = ((128, 1), (512, 4), (2048, 16))
N_GROUPS = len(DILATED_GROUPS)
ATTN_BLOCK = 128
REL_WINDOW = 128
EPS = 1e-6

kernel_name = "yoco_gmlp_dilated_alibi_macaron"


def rms_norm(x, g):
    xf = x.astype(jnp.float32)
    y = xf * lax.rsqrt(jnp.mean(xf * xf, axis=-1, keepdims=True) + EPS)
    return (y * g.astype(jnp.float32)).astype(x.dtype)


def swiglu(h, w_gate, w_up, w_down):
    return (jax.nn.silu(h @ w_gate) * (h @ w_up)) @ w_down


def gmlp_mixer(h, w_in, v_norm, w_s, b_s, w_out):
    bsz, seq, _ = h.shape
    z = jax.nn.gelu(h @ w_in)
    u, v = z[..., :GMLP_D_GATE], z[..., GMLP_D_GATE:]
    v = rms_norm(v, v_norm)
    v = v.reshape(bsz, seq // GMLP_CHUNK, GMLP_CHUNK, GMLP_GROUPS, GMLP_GROUP_WIDTH)
    causal = jnp.tril(jnp.ones((GMLP_CHUNK, GMLP_CHUNK), dtype=w_s.dtype))
    ws = w_s * causal[None]
    sv = jnp.einsum('gpq,bnqgc->bnpgc', ws, v) + b_s.T[None, None, :, :, None]
    return (u * sv.reshape(bsz, seq, GMLP_D_GATE)) @ w_out


def dilated_branch(q, k, v, dil, slopes):
    bsz, seq, nh, dh = q.shape
    L = seq // dil
    n = bsz * dil

    def to_sub(t):
        t = t.reshape(bsz, L, dil, nh, dh).transpose(0, 2, 1, 3, 4)
        return t.reshape(n, L, nh, dh)

    def from_sub(t):
        rest = t.shape[2:]
        t = t.reshape((bsz, dil, L) + rest)
        t = jnp.swapaxes(t, 1, 2)
        return t.reshape((bsz, seq) + rest)

    nb = -(-L // ATTN_BLOCK)
    Lp = nb * ATTN_BLOCK
    pad = Lp - L
    qs = jnp.pad(to_sub(q), ((0, 0), (0, pad), (0, 0), (0, 0))).reshape(n, nb, ATTN_BLOCK, nh, dh)

    def band(t):
        tp = jnp.pad(to_sub(t), ((0, 0), (ATTN_BLOCK, pad), (0, 0), (0, 0)))
        prev = tp[:, :Lp].reshape(n, nb, ATTN_BLOCK, nh, dh)
        cur = tp[:, ATTN_BLOCK:].reshape(n, nb, ATTN_BLOCK, nh, dh)
        return jnp.concatenate([prev, cur], axis=2)

    kb, vb = band(k), band(v)
    s = jnp.einsum('nbqhd,nbkhd->nbhqk', qs, kb, preferred_element_type=jnp.float32)
    qi = jnp.arange(ATTN_BLOCK)[:, None]
    kj = jnp.arange(2 * ATTN_BLOCK)[None, :]
    delta = qi + ATTN_BLOCK - kj
    j_abs = jnp.arange(nb)[:, None, None] * ATTN_BLOCK - ATTN_BLOCK + kj[None]
    valid = (delta >= 0)[None] & (delta <= REL_WINDOW)[None] & (j_abs >= 0)
    alibi = -slopes[:, None, None] * (delta * dil).astype(jnp.float32)[None]
    s = jnp.where(valid[None, :, None], s + alibi[None, None], -jnp.inf)
    m = jnp.max(s, axis=-1, keepdims=True)
    p = jnp.exp(s - m)
    l = jnp.sum(p, axis=-1, keepdims=True)
    o = jnp.einsum('nbhqk,nbkhd->nbqhd', p / l, vb.astype(jnp.float32))
    lse = (m + jnp.log(l))[..., 0]
    lse = lse.transpose(0, 1, 3, 2).reshape(n, Lp, nh)[:, :L]
    o = o.reshape(n, Lp, nh, dh)[:, :L]
    return from_sub(o), from_sub(lse)


def dilated_mixer(h, k_sh, v_sh, w_q, q_norm, w_o, slopes):
    bsz, seq, _ = h.shape
    q = (h @ w_q).reshape(bsz, seq, N_GROUPS, N_HEADS, HEAD_DIM)
    q = rms_norm(q, q_norm[:, None, :]) * (HEAD_DIM ** -0.5)
    outs, lses = [], []
    for g, (_, dil) in enumerate(DILATED_GROUPS):
        o, lse = dilated_branch(q[:, :, g], k_sh[:, :, g], v_sh[:, :, g], dil, slopes)
        outs.append(o)
        lses.append(lse)
    wts = jax.nn.softmax(jnp.stack(lses, 0), axis=0)
    o = jnp.sum(wts[..., None] * jnp.stack(outs, 0), axis=0)
    return o.astype(h.dtype).reshape(bsz, seq, N_HEADS * HEAD_DIM) @ w_o


def setup_inputs(seed: int = 0) -> dict:
    key = jax.random.key(seed)
    ks = iter(jax.random.split(key, 32))

    def nrm(shape, scale):
        return jax.random.normal(next(ks), shape, dtype=jnp.float32) * scale

    def gain(shape):
        return 1.0 + nrm(shape, 0.02)

    D, F = D_MODEL, D_FF
    qkv_w = N_GROUPS * N_HEADS * HEAD_DIM
    return {
        "x": nrm((BATCH, SEQ, D), 1.0),
        "ffn1_norm": gain((DEPTH, D)),
        "ffn1_w_gate": nrm((DEPTH, D, F), D ** -0.5),
        "ffn1_w_up": nrm((DEPTH, D, F), D ** -0.5),
        "ffn1_w_down": nrm((DEPTH, F, D), F ** -0.5),
        "mix_norm": gain((DEPTH, D)),
        "ffn2_norm": gain((DEPTH, D)),
        "ffn2_w_gate": nrm((DEPTH, D, F), D ** -0.5),
        "ffn2_w_up": nrm((DEPTH, D, F), D ** -0.5),
        "ffn2_w_down": nrm((DEPTH, F, D), F ** -0.5),
        "gmlp_w_in": nrm((N_A_LAYERS, D, 2 * GMLP_D_GATE), D ** -0.5),
        "gmlp_v_norm": gain((N_A_LAYERS, GMLP_D_GATE)),
        "gmlp_w_s": nrm((N_A_LAYERS, GMLP_GROUPS, GMLP_CHUNK, GMLP_CHUNK), GMLP_CHUNK ** -0.5),
        "gmlp_b_s": 1.0 + nrm((N_A_LAYERS, GMLP_GROUPS, GMLP_CHUNK), 0.02),
        "gmlp_w_out": nrm((N_A_LAYERS, GMLP_D_GATE, D), GMLP_D_GATE ** -0.5),
        "kv_norm": gain((D,)),
        "w_kv": nrm((D, 2 * qkv_w), D ** -0.5),
        "k_norm": gain((N_GROUPS, HEAD_DIM)),
        "attn_w_q": nrm((N_B_LAYERS, D, qkv_w), D ** -0.5),
        "attn_q_norm": gain((N_B_LAYERS, N_GROUPS, HEAD_DIM)),
        "attn_w_o": nrm((N_B_LAYERS, N_HEADS * HEAD_DIM, D), (N_HEADS * HEAD_DIM) ** -0.5),
    }


def reference(x, ffn1_norm, ffn1_w_gate, ffn1_w_up, ffn1_w_down, mix_norm,
              ffn2_norm, ffn2_w_gate, ffn2_w_up, ffn2_w_down,
              gmlp_w_in, gmlp_v_norm, gmlp_w_s, gmlp_b_s, gmlp_w_out,
              kv_norm, w_kv, k_norm, attn_w_q, attn_q_norm, attn_w_o):
    bsz, seq, _ = x.shape
    slopes = jnp.exp2(-8.0 * jnp.arange(1, N_HEADS + 1, dtype=jnp.float32) / N_HEADS)
    k_sh = v_sh = None
    for l in range(DEPTH):
        x = x + 0.5 * swiglu(rms_norm(x, ffn1_norm[l]), ffn1_w_gate[l], ffn1_w_up[l], ffn1_w_down[l])
        h = rms_norm(x, mix_norm[l])
        if l < N_A_LAYERS:
            x = x + gmlp_mixer(h, gmlp_w_in[l], gmlp_v_norm[l], gmlp_w_s[l], gmlp_b_s[l], gmlp_w_out[l])
        else:
            j = l - N_A_LAYERS
            x = x + dilated_mixer(h, k_sh, v_sh, attn_w_q[j], attn_q_norm[j], attn_w_o[j], slopes)
        x = x + 0.5 * swiglu(rms_norm(x, ffn2_norm[l]), ffn2_w_gate[l], ffn2_w_up[l], ffn2_w_down[l])
        if l == N_A_LAYERS - 1:
            kv = (rms_norm(x, kv_norm) @ w_kv).reshape(bsz, seq, 2, N_GROUPS, N_HEADS, HEAD_DIM)
            k_sh = rms_norm(kv[:, :, 0], k_norm[:, None, :])
            v_sh = kv[:, :, 1]
    return x
```

```python
import numpy as np
import ml_dtypes
from contextlib import ExitStack
import concourse.bass as bass
import concourse.mybir as mybir
from concourse.bass_utils import run_bass_kernel_spmd

F32 = mybir.dt.float32
BF16 = mybir.dt.bfloat16
AF = mybir.ActivationFunctionType
ALU = mybir.AluOpType
NPBF = ml_dtypes.bfloat16

D = 2048
FF = 5632
NT = 512
KC = 16
FC = 44
TOK = 2048
NTILE = 4
EPS = 1e-6
NSLOT = 4
PC = 256
NHEAD = 16
DILS = (1, 4, 16)
PERM_R = (1, 4, 4)
NGCOL = 217
BIGD = 1.0e9
NEGH = -1.0e9
WBT = (0, 1, 2, 1, 2, 0, 1, 2)


class Op:
    __slots__ = ("eng", "fn", "deps", "dma", "signal", "count", "idx", "inc")


class Sched:
    def __init__(self, null=False):
        self.null = null
        self.ops = []
        self.res = {}
        self.dma_cnt = {}

    def op(self, eng, fn, reads=(), writes=(), dma=None, inc=16):
        if self.null:
            return None
        o = Op()
        o.inc = inc
        o.eng = eng
        o.fn = fn
        o.dma = dma
        o.signal = False
        o.count = 0
        o.idx = len(self.ops)
        key = ("dma", dma) if dma else eng
        deps = {}
        for r in reads:
            st = self.res.get(r)
            if st is None:
                st = [None, {}]
                self.res[r] = st
            if st[0] is not None:
                deps[st[0].idx] = st[0]
            st[1][key] = o
        for w in writes:
            st = self.res.get(w)
            if st is None:
                st = [None, {}]
                self.res[w] = st
            if st[0] is not None:
                deps[st[0].idx] = st[0]
            for ro in st[1].values():
                if ro is not o:
                    deps[ro.idx] = ro
            st[0] = o
            st[1] = {}
        dl = []
        for d in deps.values():
            if d is o:
                continue
            if d.dma is None and o.dma is None and d.eng == "pe" and o.eng == "pe":
                continue
            dl.append(d)
            if d.dma is None:
                d.signal = True
        o.deps = dl
        if dma:
            c = self.dma_cnt.get(dma, 0) + inc
            self.dma_cnt[dma] = c
            o.count = c
        self.ops.append(o)
        return o

    def emit(self, nc, stack):
        engs = ["pe", "act", "dve", "pool", "sp"]
        sem = {e: stack.enter_context(nc.semaphore("s_" + e)) for e in engs}
        dsem = {n: stack.enter_context(nc.semaphore("d_" + n)) for n in self.dma_cnt}
        cnt = {e: 0 for e in engs}
        per = {e: [] for e in engs}
        for o in self.ops:
            if o.dma is None and o.signal:
                cnt[o.eng] += 1
                o.count = cnt[o.eng]
            per[o.eng].append(o)
        final = dict(self.dma_cnt)

        WIN, MARGIN = 64, 200

        def run(ename, e):
            known = {}
            lst = per[ename]
            for p, o in enumerate(lst):
                need = {}
                for d in o.deps:
                    k = ("dma", d.dma) if d.dma else ("eng", d.eng)
                    if need.get(k, 0) < d.count:
                        need[k] = d.count
                if ename == "pe" and need:
                    for q in range(p + 1, min(len(lst), p + 1 + WIN)):
                        for d in lst[q].deps:
                            if d.dma is None and d.idx < o.idx - MARGIN:
                                k = ("eng", d.eng)
                                if k in need and need[k] < d.count:
                                    need[k] = d.count
                for k, v in need.items():
                    if known.get(k, 0) >= v:
                        continue
                    known[k] = v
                    e.wait_ge(dsem[k[1]] if k[0] == "dma" else sem[k[1]], v)
                ins = o.fn(e)
                if o.dma:
                    ins.then_inc(dsem[o.dma], o.inc)
                elif o.signal:
                    ins.then_inc(sem[ename], 1)
            if ename == "sp":
                for n, v in final.items():
                    e.wait_ge(dsem[n], v)

        with nc.Block() as block:
            @block.tensor
            def _(e):
                run("pe", e)

            @block.scalar
            def _(e):
                run("act", e)

            @block.vector
            def _(e):
                run("dve", e)

            @block.gpsimd
            def _(e):
                run("pool", e)

            @block.sync
            def _(e):
                run("sp", e)


class WStream:
    def __init__(self, S, slots, pieces=None, wbf=None):
        self.S = S
        self.slots = slots
        self.record = pieces is None
        self.pieces = [] if pieces is None else pieces
        self.wbf = wbf
        self.i = 0
        self.issued = 0
        self.ph = None
        self.t = 0
        self.j = 0
        self.count = {}

    def begin_tile(self, ph, t):
        self.ph, self.t, self.j = ph, t, 0

    def next(self, src, nk):
        i = self.i
        self.i += 1
        if self.record:
            self.pieces.append((src, nk, self.ph, self.t, self.j))
            self.j += 1
            self.count[self.ph] = max(self.count.get(self.ph, 0), self.j)
            return None, i % NSLOT
        lim = min(len(self.pieces), i + NSLOT - 1)
        while self.issued < lim:
            j = self.issued
            s = j % NSLOT
            sr, n, ph, t, pj = self.pieces[j]
            slot = self.slots[s]
            if self.wbf is None:
                self.S.op("pool", lambda e, sr=sr, n=n, slot=slot: e.dma_start(out=slot[:, 0:n, :], in_=sr),
                          writes=["w:%d" % s], dma="w%d" % s)
            elif t < WBT[pj % 8]:
                self.S.op("pool", lambda e, sr=sr, n=n, slot=slot: e.dma_start(out=slot[:, 0:n, :], in_=sr),
                          writes=["w:%d" % s], dma="w%d" % s)
            elif t == WBT[pj % 8]:
                dst = self.wbf[ph][pj][:, 0:n * PC]
                self.S.op("pool", lambda e, sr=sr, n=n, slot=slot: e.dma_start(out=slot[:, 0:n, :], in_=sr),
                          writes=["w:%d" % s], dma="w%d" % s)
                self.S.op("sp", lambda e, n=n, slot=slot, dst=dst: e.dma_start(
                    out=dst, in_=slot[:, 0:n, :].rearrange("p k c -> p (k c)")),
                    reads=["w:%d" % s], writes=["wbf:%s:%d" % (ph, pj)], dma="wb%d" % s)
            else:
                src2 = self.wbf[ph][pj][:, 0:n * PC]
                self.S.op("pool", lambda e, n=n, slot=slot, src2=src2: e.dma_start(
                    out=slot[:, 0:n, :].rearrange("p k c -> p (k c)"), in_=src2),
                    reads=["wbf:%s:%d" % (ph, pj)], writes=["w:%d" % s], dma="w%d" % s)
            self.issued += 1
        return self.slots[i % NSLOT], i % NSLOT


class Ring:
    def __init__(self, items):
        self.items = list(items)
        self.i = 0

    def next(self):
        v = self.items[self.i % len(self.items)]
        self.i += 1
        return v


def wpiece(wmat, k0, nk, c0):
    return wmat.rearrange("(k p) f -> p k f", p=128)[:, k0:k0 + nk, c0:c0 + PC]


def build(ntile=NTILE):
    nc = bass.Bass("TRN2", target_bir_lowering=False)
    T = {}

    def din(name, shape, dt=F32):
        T[name] = nc.dram_tensor(name, list(shape), dt, kind="ExternalInput").ap()

    def dout(name, shape, dt=F32):
        T[name] = nc.dram_tensor(name, list(shape), dt, kind="ExternalOutput").ap()

    def dint(name, shape, dt=F32):
        T[name] = nc.dram_tensor(name, list(shape), dt).ap()

    NL = 4
    for nm in ("ffn1", "ffn2"):
        din(nm + "_w_gate", [NL, D, FF])
        din(nm + "_w_up", [NL, D, FF])
        din(nm + "_w_down", [NL, FF, D])
    din("gains", [128, NGCOL])
    din("xin", [D, TOK])
    din("gmlp_w_in", [2, D, 2 * D])
    din("gmlp_w_out", [2, D, D])
    din("w_kv", [D, 6 * D])
    din("wsT", [2, 128, 16 * 128])
    din("vgain", [2, D])
    din("bs", [2, D])
    din("cmask", [128, 128])
    din("attn_w_q", [2, D, 3 * D])
    din("attn_w_o", [2, D, D])
    din("dtiles", [9, 128, 128])
    din("nhm", [128, 1])
    dout("xout", [D, TOK])
    dint("xmid", [D, TOK])
    for t in range(ntile):
        for g in range(3):
            dint("KTt%dg%d" % (t, g), [16, 128, NT], BF16)
            dint("Vt%dg%d" % (t, g), [NT, D], BF16)
            if g == 2 or t == ntile - 1:
                dint("KTg%dg%d" % (t, g), [2 * 16, 128, NT], BF16)
                dint("Vg%dg%d" % (t, g), [2 * NT, D], BF16)

    stack = ExitStack()

    def sb(name, shape, dt=F32):
        return stack.enter_context(nc.sbuf_tensor("sb_" + name, list(shape), dt))

    with stack:
        x = sb("x", [128, KC, NT], F32)
        h = sb("h", [128, KC, NT], BF16)
        big = sb("big", [128, 48, NT], BF16)
        wsl = [sb("w%d" % i, [128, KC, PC], BF16) for i in range(NSLOT)]
        gains = sb("gains", [128, NGCOL], F32)
        ones32 = sb("ones32", [128, 128], F32)
        onesbf = sb("onesbf", [128, 128], BF16)
        epsc = sb("epsc", [128, 1], F32)
        eps128 = sb("eps128", [128, 1], F32)
        zeroc = sb("zeroc", [128, 1], F32)
        sq = [sb("sq%d" % i, [128, NT], F32) for i in range(2)]
        rstd = [sb("rstd%d" % i, [128, NT], F32) for i in range(2)]
        lnt = [sb("lnt%d" % i, [128, NT], F32) for i in range(1)]
        sg = [sb("sg%d" % i, [128, NT], BF16) for i in range(2)]
        ps = [stack.enter_context(nc.psum_tensor("ps%d" % i, [128, NT], F32)) for i in range(8)]
        vgain = sb("vgain", [128, D], F32)
        bsb = sb("bsb", [128, 16, 128], F32)
        wsTm = sb("wsTm", [128, 16, 128], BF16)
        cmask = sb("cmask", [128, 128], F32)
        ssqv = sb("ssqv", [128, 4], F32)
        rsv = sb("rsv", [128, 4], F32)
        lnv = sb("lnv", [128, 4], F32)
        KW = 640 + 1024 + 2560
        kbuf = [sb("kbuf%d" % i, [128, KW], BF16) for i in range(2)]
        vbuf = [sb("vbuf%d" % i, [128, 33, 128], BF16) for i in range(2)]
        dtl = sb("dtl", [128, 9, 128], F32)
        nhm = sb("nhm", [128, 1], F32)
        tsc = [sb("tsc%d" % i, [128, NT], F32) for i in range(2)]
        pT = [sb("pT%d" % i, [128, NT], BF16) for i in range(3)]
        tadd = tsc
        kst = pT
        rl = sq

        def emit_all(S, W):
            bank = Ring(range(8))
            sqr = Ring(range(2))
            rsr = Ring(range(2))
            lnr = Ring(range(1))
            sgr = Ring(range(2))

            S.op("sp", lambda e: e.dma_start(out=gains[:], in_=T["gains"]), writes=["gains"], dma="c0")
            S.op("dve", lambda e: e.memset(ones32[:], 1.0), writes=["ones32"])
            S.op("dve", lambda e: e.memset(onesbf[:], 1.0), writes=["onesbf"])
            S.op("dve", lambda e: e.memset(epsc[:], EPS), writes=["epsc"])
            S.op("dve", lambda e: e.memset(eps128[:], EPS * 128.0), writes=["eps128"])
            S.op("dve", lambda e: e.memset(zeroc[:], 0.0), writes=["zeroc"])
            S.op("sp", lambda e: e.dma_start(out=cmask[:], in_=T["cmask"]), writes=["cmask"], dma="c1")
            S.op("sp", lambda e: e.dma_start(out=dtl[:], in_=T["dtiles"].rearrange("n p q -> p n q")),
                 writes=["dtl"], dma="c2")
            S.op("sp", lambda e: e.dma_start(out=nhm[:], in_=T["nhm"]), writes=["nhm"], dma="c3")

            XR = ["x:%d" % c for c in range(KC)]
            HR = ["h:%d" % c for c in range(KC)]

            def rmsnorm(gcol):
                b = bank.next()
                for c in range(KC):
                    si = sgr.next()
                    S.op("act", lambda e, c=c, si=si: e.activation(out=sg[si][:], in_=x[:, c, :], func=AF.Square),
                         reads=["x:%d" % c], writes=["sg:%d" % si])
                    S.op("pe", lambda e, c=c, si=si, b=b: e.matmul(ps[b][:], onesbf[:], sg[si][:],
                                                                    start=(c == 0), stop=(c == KC - 1)),
                         reads=["sg:%d" % si, "onesbf"], writes=["ps:%d" % b])
                ri = rsr.next()
                S.op("act", lambda e, b=b, ri=ri: e.activation(out=lnt[0][:], in_=ps[b][:], func=AF.Ln,
                                                                bias=epsc[:, 0:1], scale=1.0 / D),
                     reads=["ps:%d" % b, "epsc"], writes=["lnt:0"])
                S.op("act", lambda e, ri=ri: e.activation(out=rstd[ri][:], in_=lnt[0][:], func=AF.Exp, scale=-0.5),
                     reads=["lnt:0"], writes=["rstd:%d" % ri])
                for c in range(KC):
                    S.op("dve", lambda e, c=c, ri=ri: e.scalar_tensor_tensor(
                        out=h[:, c, :], in0=x[:, c, :], scalar=gains[:, gcol + c:gcol + c + 1], in1=rstd[ri][:],
                        op0=ALU.mult, op1=ALU.mult),
                        reads=["x:%d" % c, "rstd:%d" % ri, "gains"], writes=["h:%d" % c])

            def ffn(nm, l, gcol, after_x=None):
                rmsnorm(gcol)
                wg, wu, wd = T[nm + "_w_gate"][l], T[nm + "_w_up"][l], T[nm + "_w_down"][l]
                for fj in range(FC // 2):
                    pg, sgi = W.next(wpiece(wg, 0, KC, fj * PC), KC)
                    pu, sui = W.next(wpiece(wu, 0, KC, fj * PC), KC)
                    grp = []
                    for cc in range(2):
                        grp.append((pg, sgi, cc, bank.next()))
                        grp.append((pu, sui, cc, bank.next()))
                    if fj == 0:
                        for k in range(KC):
                            for (pw, swi, cc, b) in grp:
                                S.op("pe", lambda e, k=k, cc=cc, pw=pw, b=b: e.matmul(
                                    ps[b][:], pw[:, k, cc * 128:(cc + 1) * 128], h[:, k, :], start=(k == 0), stop=(k == KC - 1)),
                                    reads=["w:%d" % swi, "h:%d" % k], writes=["ps:%d" % b])
                    else:
                        for (pw, swi, cc, b) in grp:
                            for k in range(KC):
                                S.op("pe", lambda e, k=k, cc=cc, pw=pw, b=b: e.matmul(
                                    ps[b][:], pw[:, k, cc * 128:(cc + 1) * 128], h[:, k, :], start=(k == 0), stop=(k == KC - 1)),
                                    reads=["w:%d" % swi, "h:%d" % k], writes=["ps:%d" % b])
                    for cc in range(2):
                        f = 2 * fj + cc
                        bg, bu = grp[2 * cc][3], grp[2 * cc + 1][3]
                        gi = sgr.next()
                        S.op("act", lambda e, bg=bg, gi=gi: e.activation(out=sg[gi][:], in_=ps[bg][:], func=AF.Silu),
                             reads=["ps:%d" % bg], writes=["sg:%d" % gi])
                        S.op("dve", lambda e, bu=bu, gi=gi, f=f: e.tensor_tensor(
                            out=big[:, f, :], in0=ps[bu][:], in1=sg[gi][:], op=ALU.mult),
                            reads=["ps:%d" % bu, "sg:%d" % gi], writes=["big:%d" % f])
                for cb in range(D // PC):
                    bb = [bank.next(), bank.next()]
                    for kg in range(3):
                        nk = KC if kg < 2 else FC - 2 * KC
                        pd, sdi = W.next(wpiece(wd, kg * KC, nk, cb * PC), nk)
                        for cc in range(2):
                            for k in range(nk):
                                kk = kg * KC + k
                                S.op("pe", lambda e, k=k, kk=kk, cc=cc, pd=pd, b=bb[cc]: e.matmul(
                                    ps[b][:], pd[:, k, cc * 128:(cc + 1) * 128], big[:, kk, :],
                                    start=(kk == 0), stop=(kk == FC - 1)),
                                    reads=["w:%d" % sdi, "big:%d" % kk], writes=["ps:%d" % bb[cc]])
                    for cc in range(2):
                        dch = 2 * cb + cc
                        S.op("dve", lambda e, dch=dch, b=bb[cc]: e.scalar_tensor_tensor(
                            out=x[:, dch, :], in0=ps[b][:], scalar=0.5, in1=x[:, dch, :], op0=ALU.mult, op1=ALU.add),
                            reads=["ps:%d" % bb[cc], "x:%d" % dch], writes=["x:%d" % dch])
                        if after_x is not None:
                            after_x(dch)

            def proj_residual(wmat, src_chunks, src_res):
                for cb in range(D // PC):
                    pw, si = W.next(wpiece(wmat, 0, KC, cb * PC), KC)
                    for cc in range(2):
                        b = bank.next()
                        dch = 2 * cb + cc
                        for k in range(KC):
                            S.op("pe", lambda e, k=k, cc=cc, pw=pw, b=b: e.matmul(
                                ps[b][:], pw[:, k, cc * 128:(cc + 1) * 128], src_chunks(k), start=(k == 0), stop=(k == KC - 1)),
                                reads=["w:%d" % si, src_res(k)], writes=["ps:%d" % b])
                        S.op("dve", lambda e, dch=dch, b=b: e.tensor_tensor(
                            out=x[:, dch, :], in0=ps[b][:], in1=x[:, dch, :], op=ALU.add),
                            reads=["ps:%d" % b, "x:%d" % dch], writes=["x:%d" % dch])

            def headnorm_proj(wmat, col0, nchunks, gain_col_of, out_writer, eps_tile, eps_res):
                pending = []

                def post(ch, b, si):
                    b2 = bank.next()
                    S.op("pe", lambda e, b2=b2, si=si: e.matmul(ps[b2][:], onesbf[:], sg[si][:], start=True, stop=True),
                         reads=["sg:%d" % si, "onesbf"], writes=["ps:%d" % b2])
                    ri = rsr.next()
                    S.op("act", lambda e, b2=b2, ri=ri: e.activation(out=lnt[0][:], in_=ps[b2][:], func=AF.Ln,
                                                                      bias=eps_tile[:, 0:1], scale=1.0 / 128.0 if eps_res == "epsc" else 1.0),
                         reads=["ps:%d" % b2, eps_res], writes=["lnt:0"])
                    S.op("act", lambda e, ri=ri: e.activation(out=rstd[ri][:], in_=lnt[0][:], func=AF.Exp, scale=-0.5),
                         reads=["lnt:0"], writes=["rstd:%d" % ri])
                    out_writer(ch, b, ri)

                pj = 0
                while pj < nchunks // 2:
                    npc = 2 if pj == 0 else 1
                    grp = []
                    for q_ in range(npc):
                        pw, si_w = W.next(wpiece(wmat, 0, KC, col0 + (pj + q_) * PC), KC)
                        for cc in range(2):
                            grp.append((pw, si_w, cc, bank.next(), 2 * (pj + q_) + cc))
                    if npc == 2:
                        order = [(k, gi) for k in range(KC) for gi in range(len(grp))]
                    else:
                        order = [(k, gi) for gi in range(len(grp)) for k in range(KC)]
                    done = set()
                    for (k, gi) in order:
                        pw, si_w, cc, b, ch = grp[gi]
                        S.op("pe", lambda e, k=k, cc=cc, pw=pw, b=b: e.matmul(
                            ps[b][:], pw[:, k, cc * 128:(cc + 1) * 128], h[:, k, :], start=(k == 0), stop=(k == KC - 1)),
                            reads=["w:%d" % si_w, "h:%d" % k], writes=["ps:%d" % b])
                        if npc == 1 and k == KC - 1:
                            si = sgr.next()
                            S.op("act", lambda e, b=b, si=si: e.activation(out=sg[si][:], in_=ps[b][:], func=AF.Square),
                                 reads=["ps:%d" % b], writes=["sg:%d" % si])
                            if pending:
                                post(*pending.pop(0))
                            pending.append((ch, b, si))
                    if npc == 2:
                        for (pw, si_w, cc, b, ch) in grp:
                            si = sgr.next()
                            S.op("act", lambda e, b=b, si=si: e.activation(out=sg[si][:], in_=ps[b][:], func=AF.Square),
                                 reads=["ps:%d" % b], writes=["sg:%d" % si])
                            post(ch, b, si)
                    pj += npc
                while pending:
                    post(*pending.pop(0))

            def perm_view(ap, g):
                if g == 0:
                    return ap
                return ap.rearrange("p (j r) -> p r j", r=PERM_R[g])

            def perm_dst(ap, g):
                if g == 0:
                    return ap
                return ap.rearrange("p (r j) -> p r j", r=PERM_R[g])

            def gmlp(l, gcol):
                rmsnorm(gcol)
                S.op("sp", lambda e: e.dma_start(out=vgain[:], in_=T["vgain"][l].partition_broadcast(128)),
                     writes=["vgain"], dma="g0")
                S.op("sp", lambda e: e.dma_start(out=bsb[:].rearrange("p g q -> p (g q)"),
                                                 in_=T["bs"][l].partition_broadcast(128)),
                     writes=["bsb"], dma="g1")
                S.op("pool", lambda e: e.dma_start(out=wsTm[:].rearrange("p g q -> p (g q)"), in_=T["wsT"][l]),
                     writes=["wsTm"], dma="g2")
                S.op("dve", lambda e: e.tensor_tensor(out=wsTm[:], in0=wsTm[:],
                                                      in1=cmask[:].unsqueeze(1).broadcast_to([128, 16, 128]), op=ALU.mult),
                     reads=["wsTm", "cmask"], writes=["wsTm"])
                win = T["gmlp_w_in"][l]
                gv = big[:, 16:32, :].rearrange("p c t -> p (c t)").rearrange("p (b f) -> p b f", b=4)
                for pj in range(8):
                    pw, si = W.next(wpiece(win, 0, KC, D + pj * PC), KC)
                    bks = [bank.next() for _ in range(4)]
                    if pj == 0:
                        order = [(k, blk) for k in range(KC) for blk in range(4)]
                    else:
                        order = [(k, blk) for blk in range(4) for k in range(KC)]
                    for (k, blk) in order:
                        b = bks[blk]
                        S.op("pe", lambda e, k=k, blk=blk, pw=pw, b=b: e.matmul(
                            ps[b][:, 0:PC], h[:, k, blk * 128:(blk + 1) * 128], pw[:, k, :], start=(k == 0), stop=(k == KC - 1)),
                            reads=["w:%d" % si, "h:%d" % k], writes=["ps:%d" % b])
                    for blk in range(4):
                        b = bks[blk]
                        ch = 16 + 4 * blk + pj // 2
                        S.op("act", lambda e, b=b, blk=blk, pj=pj: e.activation(
                            out=gv[:, blk, pj * PC:(pj + 1) * PC], in_=ps[b][:, 0:PC], func=AF.Gelu_apprx_tanh),
                            reads=["ps:%d" % b], writes=["big:%d" % ch])
                S.op("dve", lambda e: e.memset(ssqv[:], 0.0), writes=["ssqv"])
                junk = big[:, 32:36, :].rearrange("p c t -> p (c t)")
                for blk in range(4):
                    S.op("act", lambda e, blk=blk: e.activation(out=junk, in_=gv[:, blk, :], func=AF.Square,
                                                                accum_out=ssqv[:, blk:blk + 1]),
                         reads=["big:%d" % (16 + 4 * blk + i) for i in range(4)] + [],
                         writes=["big:32", "big:33", "big:34", "big:35", "ssqv"])
                S.op("act", lambda e: e.activation(out=lnv[:], in_=ssqv[:], func=AF.Ln, bias=epsc[:, 0:1], scale=1.0 / D),
                     reads=["ssqv", "epsc"], writes=["lnv"])
                S.op("act", lambda e: e.activation(out=rsv[:], in_=lnv[:], func=AF.Exp, scale=-0.5),
                     reads=["lnv"], writes=["rsv"])
                for blk in range(4):
                    rr = ["big:%d" % (16 + 4 * blk + i) for i in range(4)]
                    S.op("dve", lambda e, blk=blk: e.scalar_tensor_tensor(
                        out=gv[:, blk, :], in0=gv[:, blk, :], scalar=rsv[:, blk:blk + 1], in1=vgain[:],
                        op0=ALU.mult, op1=ALU.mult),
                        reads=rr + ["rsv", "vgain"], writes=rr)
                pj = 0
                while pj < 8:
                    npc = 1
                    grp = []
                    for q_ in range(npc):
                        pw, si = W.next(wpiece(win, 0, KC, (pj + q_) * PC), KC)
                        for cc in range(2):
                            grp.append((pw, si, cc, bank.next(), 2 * (pj + q_) + cc))
                    if npc == 2:
                        order = [(k, gi) for k in range(KC) for gi in range(len(grp))]
                    else:
                        order = [(k, gi) for gi in range(len(grp)) for k in range(KC)]
                    for (k, gi) in order:
                        pw, si, cc, b, f = grp[gi]
                        S.op("pe", lambda e, k=k, cc=cc, pw=pw, b=b: e.matmul(
                            ps[b][:], pw[:, k, cc * 128:(cc + 1) * 128], h[:, k, :], start=(k == 0), stop=(k == KC - 1)),
                            reads=["w:%d" % si, "h:%d" % k], writes=["ps:%d" % b])
                    for (pw, si, cc, b, f) in grp:
                        S.op("act", lambda e, b=b, f=f: e.activation(out=big[:, f, :], in_=ps[b][:], func=AF.Gelu_apprx_tanh),
                             reads=["ps:%d" % b], writes=["big:%d" % f])
                    pj += npc
                tr = Ring(range(2))
                for g in range(16):
                    b = bank.next()
                    for blk in range(4):
                        S.op("pe", lambda e, g=g, blk=blk, b=b: e.matmul(
                            ps[b][:, blk * 128:(blk + 1) * 128], gv[:, blk, g * 128:(g + 1) * 128], wsTm[:, g, :],
                            start=True, stop=True),
                            reads=["big:%d" % (16 + 4 * blk + g // 4), "wsTm"], writes=["ps:%d" % b])
                    ti = tr.next()
                    S.op("dve", lambda e, g=g, b=b, ti=ti: e.tensor_tensor(
                        out=tadd[ti][:].rearrange("p (b q) -> p b q", b=4),
                        in0=ps[b][:].rearrange("p (b q) -> p b q", b=4),
                        in1=bsb[:, g, :].unsqueeze(1).broadcast_to([128, 4, 128]), op=ALU.add),
                        reads=["ps:%d" % b, "bsb"], writes=["tsc:%d" % ti])
                    S.op("dve", lambda e, g=g, ti=ti: e.tensor_tensor(
                        out=big[:, 32 + g, :], in0=tadd[ti][:], in1=big[:, g, :], op=ALU.mult),
                        reads=["tsc:%d" % ti, "big:%d" % g], writes=["big:%d" % (32 + g)])
                proj_residual(T["gmlp_w_out"][l], lambda k: big[:, 32 + k, :], lambda k: "big:%d" % (32 + k))

            def kvproj(t, after_norm=None):
                rmsnorm(192)
                if after_norm is not None:
                    after_norm()
                wkv = T["w_kv"]
                kr = Ring(range(3))

                def kwriter(ch, b, ri):
                    g = ch // 16
                    ki = kr.next()
                    S.op("dve", lambda e, b=b, ri=ri, g=g, ki=ki: e.scalar_tensor_tensor(
                        out=perm_dst(kst[ki][:], g), in0=perm_view(ps[b][:], g), scalar=gains[:, 208 + g:209 + g],
                        in1=perm_view(rstd[ri][:], g), op0=ALU.mult, op1=ALU.mult),
                        reads=["ps:%d" % b, "rstd:%d" % ri, "gains"], writes=["pt:%d" % ki])
                    S.op("sp", lambda e, ch=ch, ki=ki: e.dma_start(out=T["KTt%dg%d" % (t, ch // 16)][ch % 16], in_=kst[ki][:]),
                         reads=["pt:%d" % ki], writes=["KTt:%d:%d" % (t, ch)], dma="k%d" % ki)

                headnorm_proj(wkv, 0, 48, None, kwriter, epsc, "epsc")
                for c in range(KC):
                    S.op("act", lambda e, c=c: e.activation(out=perm_dst(big[:, 16 + c, :], 1), in_=perm_view(h[:, c, :], 1),
                                                            func=AF.Copy),
                         reads=["h:%d" % c], writes=["big:%d" % (16 + c)])
                for g in range(3):
                    base = 0 if g % 2 == 0 else 32
                    stg = big[:, base:base + 16, :].rearrange("p c t -> p (c t)").rearrange("p (b f) -> p b f", b=4)
                    for pj in range(8):
                        pw, si = W.next(wpiece(wkv, 0, KC, 3 * D + g * D + pj * PC), KC)
                        for blk in range(4):
                            b = bank.next()
                            for k in range(KC):
                                if g == 0:
                                    lt = lambda k=k, blk=blk: h[:, k, blk * 128:(blk + 1) * 128]
                                    lr = "h:%d" % k
                                else:
                                    lt = lambda k=k, blk=blk: big[:, 16 + k, blk * 128:(blk + 1) * 128]
                                    lr = "big:%d" % (16 + k)
                                S.op("pe", lambda e, k=k, lt=lt, pw=pw, b=b: e.matmul(
                                    ps[b][:, 0:PC], lt(), pw[:, k, :], start=(k == 0), stop=(k == KC - 1)),
                                    reads=["w:%d" % si, lr], writes=["ps:%d" % b])
                            ch = base + 4 * blk + pj // 2
                            eng = "act" if (blk % 2 == 0) else "dve"
                            if eng == "act":
                                S.op("act", lambda e, b=b, blk=blk, pj=pj, stg=stg: e.activation(
                                    out=stg[:, blk, pj * PC:(pj + 1) * PC], in_=ps[b][:, 0:PC], func=AF.Copy),
                                    reads=["ps:%d" % b], writes=["big:%d" % ch])
                            else:
                                S.op("dve", lambda e, b=b, blk=blk, pj=pj, stg=stg: e.tensor_copy(
                                    out=stg[:, blk, pj * PC:(pj + 1) * PC], in_=ps[b][:, 0:PC]),
                                    reads=["ps:%d" % b], writes=["big:%d" % ch])
                    S.op("sp", lambda e, g=g, stg=stg: e.dma_start(
                        out=T["Vt%dg%d" % (t, g)].rearrange("(b p) f -> p b f", p=128), in_=stg),
                        reads=["big:%d" % (base + i) for i in range(16)], writes=["Vt:%d:%d" % (t, g)], dma="v%d" % g)
                for g in range(3):
                    if not (g == 2 or t == ntile - 1):
                        continue
                    S.op("pool", lambda e, g=g: e.collective_compute(
                        "AllGather", ALU.bypass, replica_groups=[[0, 1], [2, 3], [4, 5], [6, 7]],
                        ins=[T["KTt%dg%d" % (t, g)].opt()], outs=[T["KTg%dg%d" % (t, g)].opt()]),
                        reads=["KTt:%d:%d" % (t, g * 16 + c) for c in range(16)], writes=["KTg:%d:%d" % (t, g)],
                        dma="cck%d%d" % (t, g), inc=1)
                    S.op("pool", lambda e, g=g: e.collective_compute(
                        "AllGather", ALU.bypass, replica_groups=[[0, 1], [2, 3], [4, 5], [6, 7]],
                        ins=[T["Vt%dg%d" % (t, g)].opt()], outs=[T["Vg%dg%d" % (t, g)].opt()]),
                        reads=["Vt:%d:%d" % (t, g)], writes=["Vg:%d:%d" % (t, g)], dma="ccv%d%d" % (t, g), inc=1)

            def attention(j, gcol, t):
                rmsnorm(gcol)
                wq = T["attn_w_q"][j]

                def qwriter(ch, b, ri):
                    g = ch // 16
                    S.op("dve", lambda e, b=b, ri=ri, g=g, ch=ch: e.scalar_tensor_tensor(
                        out=perm_dst(big[:, ch, :], g), in0=perm_view(ps[b][:], g),
                        scalar=gains[:, 211 + 3 * j + g:212 + 3 * j + g],
                        in1=perm_view(rstd[ri][:], g), op0=ALU.mult, op1=ALU.mult),
                        reads=["ps:%d" % b, "rstd:%d" % ri, "gains"], writes=["big:%d" % ch])

                headnorm_proj(wq, 0, 48, None, qwriter, eps128, "eps128")

                def ksrc(ch, kt):
                    g, hh = ch // 16, ch % 16
                    if kt >= 0:
                        return T["KTt%dg%d" % (kt, g)][hh], ["KTt:%d:%d" % (kt, ch)]
                    return T["KTg%dg%d" % (NTILE + kt, g)][hh], ["KTg:%d:%d" % (NTILE + kt, g)]

                def vsrc(g, kt):
                    if kt >= 0:
                        return T["Vt%dg%d" % (kt, g)], ["Vt:%d:%d" % (kt, g)]
                    return T["Vg%dg%d" % (NTILE + kt, g)], ["Vg:%d:%d" % (NTILE + kt, g)]

                def load_kv(hd):
                    bi = hd % 2
                    kb, vb = kbuf[bi], vbuf[bi]
                    c0 = hd * 128
                    kl = [(0, 0, 0, 128, t - 1, 384), (0, 1, 128, 512, t, 0),
                          (1, 0, 640, 512, t - 1, 0), (1, 1, 1152, 512, t, 0)]
                    kl += [(2, i, 1664 + i * 512, 512, t - 4 + i, 0) for i in range(5)]
                    for (g, sub, d0, n, kt, s0) in kl:
                        src, rr = ksrc(g * 16 + hd, kt)
                        S.op("sp", lambda e, d0=d0, n=n, s0=s0, src=src: e.dma_start(out=kb[:, d0:d0 + n], in_=src[:, s0:s0 + n]),
                             reads=rr, writes=["kbuf:%d:%d:%d" % (bi, g, sub)], dma="kb%dg%di%d" % (bi, g, sub))
                    vl = [(0, 0, 0, 1, t - 1, 384), (0, 1, 1, 4, t, 0),
                          (1, 0, 5, 4, t - 1, 0), (1, 1, 9, 4, t, 0)]
                    vl += [(2, i, 13 + 4 * i, 4, t - 4 + i, 0) for i in range(5)]
                    for (g, sub, b0, nb, kt, r0) in vl:
                        src, rr = vsrc(g, kt)
                        S.op("sp", lambda e, b0=b0, nb=nb, r0=r0, src=src: e.dma_start(
                            out=vb[:, b0:b0 + nb, :],
                            in_=src[r0:r0 + nb * 128, c0:c0 + 128].rearrange("(b p) d -> p b d", p=128)),
                            reads=rr, writes=["vbuf:%d:%d:%d" % (bi, g, sub)], dma="vb%dg%di%d" % (bi, g, sub))

                def units(hd):
                    u = []
                    u.append((0, 128, 1, 0, [False] * 4, [1]))
                    u.append((0, 0, 0, 1, [t == 0, False, False, False], [0, 1]))
                    u.append((1, 640 + 512, 5 + 4, 2, [False] * 4, [1]))
                    u.append((1, 640, 5, 3, [t == 0] * 4, [0]))
                    for dT in range(5):
                        halo = (t - dT) < 0
                        u.append((2, 1664 + (4 - dT) * 512, 13 + (4 - dT) * 4, 4 + dT, [halo] * 4, [4 - dT]))
                    return u

                sbank = Ring([0, 1, 2, 3])
                tr = Ring(range(2))
                pr = Ring(range(3))
                rlr = sqr
                load_kv(0)
                pend = []

                def pv(hd, g, vb0, pi, ob, lb, first, last, subs):
                    bi = hd % 2
                    vres = ["vbuf:%d:%d:%d" % (bi, g, sb_) for sb_ in subs]
                    for i in range(4):
                        if g == 0:
                            oa = lambda bk, i=i: ps[bk][:, i * 128:(i + 1) * 128]
                            ra = pT[pi][:, i * 128:(i + 1) * 128]
                        else:
                            oa = lambda bk, i=i: ps[bk][:].rearrange("p (j r) -> p r j", r=4)[:, i, :]
                            ra = pT[pi][:, i * 128:(i + 1) * 128]
                        S.op("pe", lambda e, oa=oa, ra=ra, i=i: e.matmul(
                            oa(ob), vbuf[bi][:, vb0 + i, :], ra, start=(first and i == 0), stop=last, skip_group_check=True),
                            reads=vres + ["pt:%d" % pi], writes=["ps:%d" % ob])
                        S.op("pe", lambda e, oa=oa, ra=ra, i=i: e.matmul(
                            oa(lb), onesbf[:], ra, start=(first and i == 0), stop=last, skip_group_check=True),
                            reads=["onesbf", "pt:%d" % pi], writes=["ps:%d" % lb])

                LAG = 2

                def flush_one():
                    tup = pend.pop(0)
                    pv(*tup)
                    hd_, ob_, lb_, last_ = tup[0], tup[4], tup[5], tup[7]
                    if last_:
                        ri = rlr.next()
                        S.op("act", lambda e, lb=lb_: e.activation(out=lnt[0][:], in_=ps[lb][:], func=AF.Ln),
                             reads=["ps:%d" % lb_], writes=["lnt:0"])
                        S.op("act", lambda e, ri=ri: e.activation(out=rl[ri][:], in_=lnt[0][:], func=AF.Exp, scale=-1.0),
                             reads=["lnt:0"], writes=["sq:%d" % ri])
                        S.op("dve", lambda e, ob=ob_, ri=ri, hd=hd_: e.tensor_tensor(
                            out=h[:, hd, :], in0=ps[ob][:], in1=rl[ri][:], op=ALU.mult),
                            reads=["ps:%d" % ob_, "sq:%d" % ri], writes=["h:%d" % hd_])

                for hd in range(NHEAD):
                    bi = hd % 2
                    ob, lb = (4, 5) if hd % 2 == 0 else (6, 7)
                    slope = float(2.0 ** (-8.0 * (hd + 1) / NHEAD))
                    us = units(hd)
                    for ui, (g, kc0, vb0, di, halo, subs) in enumerate(us):
                        sbk = sbank.next()
                        qch = g * 16 + hd
                        for i in range(4):
                            S.op("pe", lambda e, i=i, kc0=kc0, sbk=sbk, qch=qch, bi=bi: e.matmul(
                                ps[sbk][:, i * 128:(i + 1) * 128], kbuf[bi][:, kc0 + i * 128:kc0 + (i + 1) * 128],
                                big[:, qch, i * 128:(i + 1) * 128], start=True, stop=True),
                                reads=["kbuf:%d:%d:%d" % (bi, g, sb_) for sb_ in subs] + ["big:%d" % qch], writes=["ps:%d" % sbk])
                        ti = tr.next()
                        S.op("dve", lambda e, sbk=sbk, ti=ti, di=di, slope=slope: e.scalar_tensor_tensor(
                            out=tsc[ti][:].rearrange("p (b q) -> p b q", b=4),
                            in0=dtl[:, di, :].unsqueeze(1).broadcast_to([128, 4, 128]), scalar=-slope,
                            in1=ps[sbk][:].rearrange("p (b q) -> p b q", b=4), op0=ALU.mult, op1=ALU.add),
                            reads=["ps:%d" % sbk, "dtl"], writes=["tsc:%d" % ti])
                        pi = pr.next()
                        segs = []
                        i0 = 0
                        while i0 < 4:
                            i1 = i0
                            while i1 < 4 and halo[i1] == halo[i0]:
                                i1 += 1
                            segs.append((i0, i1, halo[i0]))
                            i0 = i1
                        for (a0, a1, hl) in segs:
                            bt, br = (nhm, "nhm") if hl else (zeroc, "zeroc")
                            S.op("act", lambda e, a0=a0, a1=a1, bt=bt, ti=ti, pi=pi: e.activation(
                                out=pT[pi][:, a0 * 128:a1 * 128], in_=tsc[ti][:, a0 * 128:a1 * 128], func=AF.Exp,
                                bias=bt[:, 0:1], scale=1.0),
                                reads=["tsc:%d" % ti, br], writes=["pt:%d" % pi])
                        pend.append((hd, g, vb0, pi, ob, lb, ui == 0, ui == len(us) - 1, subs))
                        if len(pend) > LAG:
                            flush_one()
                        if ui == LAG and hd + 1 < NHEAD:
                            load_kv(hd + 1)
                while pend:
                    flush_one()
                proj_residual(T["attn_w_o"][j], lambda k: h[:, k, :], lambda k: "h:%d" % k)

            xin_v = T["xin"].rearrange("(c p) t -> p c t", p=128)
            xmid_v = T["xmid"].rearrange("(c p) t -> p c t", p=128)
            xo_v = T["xout"].rearrange("(c p) t -> p c t", p=128)
            def store_chunk(dst_v, nm, t, c):
                S.op("sp", lambda e: e.dma_start(out=dst_v[:, c, t * NT:(t + 1) * NT], in_=x[:, c, :]),
                     reads=["x:%d" % c], writes=["%s:%d:%d" % (nm, t, c)], dma="xs%d" % (c % 8))

            def load_chunk(src_v, nm, t, c):
                rr = ["%s:%d:%d" % (nm, t, c)] if nm else []
                S.op("sp", lambda e: e.dma_start(out=x[:, c, :], in_=src_v[:, c, t * NT:(t + 1) * NT]),
                     reads=rr, writes=["x:%d" % c], dma="xl%d" % (c % 8))

            for c in range(KC):
                load_chunk(xin_v, None, 0, c)
            for t in range(ntile):
                W.begin_tile("A", t)
                for l in range(2):
                    ffn("ffn1", l, l * 48)
                    gmlp(l, l * 48 + 16)
                    ffn("ffn2", l, l * 48 + 32,
                        after_x=(lambda c, t=t: store_chunk(xmid_v, "xmid", t, c)) if l == 1 else None)

                def nxt(t=t):
                    for c in range(KC):
                        if t + 1 < ntile:
                            load_chunk(xin_v, None, t + 1, c)
                        else:
                            load_chunk(xmid_v, "xmid", 0, c)
                kvproj(t, after_norm=nxt)
            for t in range(ntile):
                W.begin_tile("B", t)
                for j in range(2):
                    gl = 2 + j

                    def fin(c, t=t):
                        store_chunk(xo_v, "xo", t, c)
                        if t + 1 < ntile:
                            load_chunk(xmid_v, "xmid", t + 1, c)
                    ffn("ffn1", gl, gl * 48)
                    attention(j, gl * 48 + 16, t)
                    ffn("ffn2", gl, gl * 48 + 32, after_x=fin if j == 1 else None)

        S0 = Sched(null=True)
        W0 = WStream(S0, wsl)
        emit_all(S0, W0)
        S1 = Sched()
        WBN = 192
        wbf = {}
        for ph, n in W0.count.items():
            tens = [nc.dram_tensor("wbf_%s%d" % (ph, q), [min(WBN, n - q * WBN), 128, KC * PC], BF16).ap()
                    for q in range((n + WBN - 1) // WBN)]
            wbf[ph] = [tens[pj // WBN][pj % WBN] for pj in range(n)]
        W1 = WStream(S1, wsl, W0.pieces, wbf)
        emit_all(S1, W1)
        assert W1.i == len(W0.pieces)
        S1.emit(nc, stack)
    return nc


def _gains(inp):
    g = np.zeros((128, NGCOL), np.float32)

    def fm(v):
        return np.ascontiguousarray(np.asarray(v, np.float32).reshape(16, 128).T)

    for l in range(4):
        g[:, l * 48:l * 48 + 16] = fm(inp["ffn1_norm"][l])
        g[:, l * 48 + 16:l * 48 + 32] = fm(inp["mix_norm"][l])
        g[:, l * 48 + 32:l * 48 + 48] = fm(inp["ffn2_norm"][l])
    g[:, 192:208] = fm(inp["kv_norm"])
    for gg in range(3):
        g[:, 208 + gg] = inp["k_norm"][gg]
    for j in range(2):
        for gg in range(3):
            g[:, 211 + 3 * j + gg] = inp["attn_q_norm"][j, gg]
    return g


def _dtiles():
    d = np.full((9, 128, 128), BIGD, np.float32)
    k = np.arange(128)[:, None]
    q = np.arange(128)[None, :]
    for gi, dil in ((0, 1), (1, 4)):
        cur = (q - k).astype(np.float32) * dil
        d[2 * gi] = np.where(q - k >= 0, cur, BIGD)
        prv = (128 + q - k).astype(np.float32) * dil
        d[2 * gi + 1] = np.where(k >= q, prv, BIGD)
    rk, jk = k % 4, k // 4
    rq, jq = q % 4, q // 4
    same = rk == rq
    for dT in range(5):
        delta = 32 * dT + jq - jk
        ok = same & (delta >= 0) & (delta <= 128)
        d[4 + dT] = np.where(ok, delta.astype(np.float32) * 16.0, BIGD)
    return d


_NC_CACHE = {}


def _get_nc(ntile=NTILE):
    if ntile not in _NC_CACHE:
        _NC_CACHE[ntile] = build(ntile)
    return _NC_CACHE[ntile]


def kernel(**inp):
    inp = {k: np.asarray(v) for k, v in inp.items()}
    x = inp["x"].astype(np.float32, copy=False)
    ncore = 8
    nc = _get_nc()
    w = {}
    for nm in ("ffn1", "ffn2"):
        for s in ("_w_gate", "_w_up", "_w_down"):
            w[nm + s] = inp[nm + s]
    w["gmlp_w_in"] = inp["gmlp_w_in"]
    w["gmlp_w_out"] = inp["gmlp_w_out"]
    w["w_kv"] = inp["w_kv"]
    w["wsT"] = np.ascontiguousarray(inp["gmlp_w_s"].transpose(0, 3, 1, 2).reshape(2, 128, 2048))
    w["vgain"] = inp["gmlp_v_norm"]
    w["bs"] = np.ascontiguousarray(inp["gmlp_b_s"].reshape(2, 2048))
    w["cmask"] = np.triu(np.ones((128, 128), np.float32))
    w["attn_w_q"] = inp["attn_w_q"]
    w["attn_w_o"] = inp["attn_w_o"]
    w["gains"] = _gains(inp)
    w["dtiles"] = _dtiles()
    in_maps = []
    for c in range(ncore):
        m = dict(w)
        m["xin"] = np.ascontiguousarray(x[c // 2, (c % 2) * TOK:(c % 2 + 1) * TOK, :].T)
        m["nhm"] = np.full((128, 1), NEGH if c % 2 == 0 else 0.0, np.float32)
        in_maps.append(m)
    res = run_bass_kernel_spmd(nc, in_maps, core_ids=list(range(ncore))).results
    out = np.empty((4, 4096, D), np.float32)
    for c in range(ncore):
        out[c // 2, (c % 2) * TOK:(c % 2 + 1) * TOK, :] = np.asarray(res[c]["xout"]).T
    return out
```

```python
import numpy as np
import ml_dtypes
from contextlib import ExitStack
import concourse.bass as bass
import concourse.mybir as mybir
from concourse.bass_utils import run_bass_kernel_spmd

F32 = mybir.dt.float32
BF16 = mybir.dt.bfloat16
AF = mybir.ActivationFunctionType
ALU = mybir.AluOpType
NPBF = ml_dtypes.bfloat16

D = 2048
FF = 5632
NT = 512
KC = 16
FC = 44
TOK = 2048
NTILE = 4
EPS = 1e-6
NSLOT = 4
PC = 256
NHEAD = 16
DILS = (1, 4, 16)
PERM_R = (1, 4, 4)
NGCOL = 217
BIGD = 1.0e9
NEGH = -1.0e9
WBT = (0, 1, 2, 1, 2, 0, 1, 2)


class Op:
    __slots__ = ("eng", "fn", "deps", "dma", "signal", "count", "idx", "inc")


class Sched:
    def __init__(self, null=False):
        self.null = null
        self.ops = []
        self.res = {}
        self.dma_cnt = {}

    def op(self, eng, fn, reads=(), writes=(), dma=None, inc=16):
        if self.null:
            return None
        o = Op()
        o.inc = inc
        o.eng = eng
        o.fn = fn
        o.dma = dma
        o.signal = False
        o.count = 0
        o.idx = len(self.ops)
        key = ("dma", dma) if dma else eng
        deps = {}
        for r in reads:
            st = self.res.get(r)
            if st is None:
                st = [None, {}]
                self.res[r] = st
            if st[0] is not None:
                deps[st[0].idx] = st[0]
            st[1][key] = o
        for w in writes:
            st = self.res.get(w)
            if st is None:
                st = [None, {}]
                self.res[w] = st
            if st[0] is not None:
                deps[st[0].idx] = st[0]
            for ro in st[1].values():
                if ro is not o:
                    deps[ro.idx] = ro
            st[0] = o
            st[1] = {}
        dl = []
        for d in deps.values():
            if d is o:
                continue
            if d.dma is None and o.dma is None and d.eng == "pe" and o.eng == "pe":
                continue
            dl.append(d)
            if d.dma is None:
                d.signal = True
        o.deps = dl
        if dma:
            c = self.dma_cnt.get(dma, 0) + inc
            self.dma_cnt[dma] = c
            o.count = c
        self.ops.append(o)
        return o

    def emit(self, nc, stack):
        engs = ["pe", "act", "dve", "pool", "sp"]
        sem = {e: stack.enter_context(nc.semaphore("s_" + e)) for e in engs}
        dsem = {n: stack.enter_context(nc.semaphore("d_" + n)) for n in self.dma_cnt}
        cnt = {e: 0 for e in engs}
        per = {e: [] for e in engs}
        for o in self.ops:
            if o.dma is None and o.signal:
                cnt[o.eng] += 1
                o.count = cnt[o.eng]
            per[o.eng].append(o)
        final = dict(self.dma_cnt)

        WIN, MARGIN = 64, 200

        def run(ename, e):
            known = {}
            lst = per[ename]
            for p, o in enumerate(lst):
                need = {}
                for d in o.deps:
                    k = ("dma", d.dma) if d.dma else ("eng", d.eng)
                    if need.get(k, 0) < d.count:
                        need[k] = d.count
                if ename == "pe" and need:
                    for q in range(p + 1, min(len(lst), p + 1 + WIN)):
                        for d in lst[q].deps:
                            if d.dma is None and d.idx < o.idx - MARGIN:
                                k = ("eng", d.eng)
                                if k in need and need[k] < d.count:
                                    need[k] = d.count
                for k, v in need.items():
                    if known.get(k, 0) >= v:
                        continue
                    known[k] = v
                    e.wait_ge(dsem[k[1]] if k[0] == "dma" else sem[k[1]], v)
                ins = o.fn(e)
                if o.dma:
                    ins.then_inc(dsem[o.dma], o.inc)
                elif o.signal:
                    ins.then_inc(sem[ename], 1)
            if ename == "sp":
                for n, v in final.items():
                    e.wait_ge(dsem[n], v)

        with nc.Block() as block:
            @block.tensor
            def _(e):
                run("pe", e)

            @block.scalar
            def _(e):
                run("act", e)

            @block.vector
            def _(e):
                run("dve", e)

            @block.gpsimd
            def _(e):
                run("pool", e)

            @block.sync
            def _(e):
                run("sp", e)


class WStream:
    def __init__(self, S, slots, pieces=None, wbf=None):
        self.S = S
        self.slots = slots
        self.record = pieces is None
        self.pieces = [] if pieces is None else pieces
        self.wbf = wbf
        self.i = 0
        self.issued = 0
        self.ph = None
        self.t = 0
        self.j = 0
        self.count = {}

    def begin_tile(self, ph, t):
        self.ph, self.t, self.j = ph, t, 0

    def next(self, src, nk):
        i = self.i
        self.i += 1
        if self.record:
            self.pieces.append((src, nk, self.ph, self.t, self.j))
            self.j += 1
            self.count[self.ph] = max(self.count.get(self.ph, 0), self.j)
            return None, i % NSLOT
        lim = min(len(self.pieces), i + NSLOT - 1)
        while self.issued < lim:
            j = self.issued
            s = j % NSLOT
            sr, n, ph, t, pj = self.pieces[j]
            slot = self.slots[s]
            if self.wbf is None:
                self.S.op("pool", lambda e, sr=sr, n=n, slot=slot: e.dma_start(out=slot[:, 0:n, :], in_=sr),
                          writes=["w:%d" % s], dma="w%d" % s)
            elif t < WBT[pj % 8]:
                self.S.op("pool", lambda e, sr=sr, n=n, slot=slot: e.dma_start(out=slot[:, 0:n, :], in_=sr),
                          writes=["w:%d" % s], dma="w%d" % s)
            elif t == WBT[pj % 8]:
                dst = self.wbf[ph][pj][:, 0:n * PC]
                self.S.op("pool", lambda e, sr=sr, n=n, slot=slot: e.dma_start(out=slot[:, 0:n, :], in_=sr),
                          writes=["w:%d" % s], dma="w%d" % s)
                self.S.op("sp", lambda e, n=n, slot=slot, dst=dst: e.dma_start(
                    out=dst, in_=slot[:, 0:n, :].rearrange("p k c -> p (k c)")),
                    reads=["w:%d" % s], writes=["wbf:%s:%d" % (ph, pj)], dma="wb%d" % s)
            else:
                src2 = self.wbf[ph][pj][:, 0:n * PC]
                self.S.op("pool", lambda e, n=n, slot=slot, src2=src2: e.dma_start(
                    out=slot[:, 0:n, :].rearrange("p k c -> p (k c)"), in_=src2),
                    reads=["wbf:%s:%d" % (ph, pj)], writes=["w:%d" % s], dma="w%d" % s)
            self.issued += 1
        return self.slots[i % NSLOT], i % NSLOT


class Ring:
    def __init__(self, items):
        self.items = list(items)
        self.i = 0

    def next(self):
        v = self.items[self.i % len(self.items)]
        self.i += 1
        return v


def wpiece(wmat, k0, nk, c0):
    return wmat.rearrange("(k p) f -> p k f", p=128)[:, k0:k0 + nk, c0:c0 + PC]


def build(ntile=NTILE):
    nc = bass.Bass("TRN2", target_bir_lowering=False)
    T = {}

    def din(name, shape, dt=F32):
        T[name] = nc.dram_tensor(name, list(shape), dt, kind="ExternalInput").ap()

    def dout(name, shape, dt=F32):
        T[name] = nc.dram_tensor(name, list(shape), dt, kind="ExternalOutput").ap()

    def dint(name, shape, dt=F32):
        T[name] = nc.dram_tensor(name, list(shape), dt).ap()

    NL = 4
    for nm in ("ffn1", "ffn2"):
        din(nm + "_w_gate", [NL, D, FF])
        din(nm + "_w_up", [NL, D, FF])
        din(nm + "_w_down", [NL, FF, D])
    din("gains", [128, NGCOL])
    din("xin", [D, TOK])
    din("gmlp_w_in", [2, D, 2 * D])
    din("gmlp_w_out", [2, D, D])
    din("w_kv", [D, 6 * D])
    din("wsT", [2, 128, 16 * 128])
    din("vgain", [2, D])
    din("bs", [2, D])
    din("cmask", [128, 128])
    din("attn_w_q", [2, D, 3 * D])
    din("attn_w_o", [2, D, D])
    din("dtiles", [9, 128, 128])
    din("nhm", [128, 1])
    dout("xout", [D, TOK])
    dint("xmid", [D, TOK])
    for t in range(ntile):
        for g in range(3):
            dint("KTt%dg%d" % (t, g), [16, 128, NT], BF16)
            dint("Vt%dg%d" % (t, g), [NT, D], BF16)
            if g == 2 or t == ntile - 1:
                dint("KTg%dg%d" % (t, g), [2 * 16, 128, NT], BF16)
                dint("Vg%dg%d" % (t, g), [2 * NT, D], BF16)

    stack = ExitStack()

    def sb(name, shape, dt=F32):
        return stack.enter_context(nc.sbuf_tensor("sb_" + name, list(shape), dt))

    with stack:
        x = sb("x", [128, KC, NT], F32)
        h = sb("h", [128, KC, NT], BF16)
        big = sb("big", [128, 48, NT], BF16)
        wsl = [sb("w%d" % i, [128, KC, PC], BF16) for i in range(NSLOT)]
        gains = sb("gains", [128, NGCOL], F32)
        ones32 = sb("ones32", [128, 128], F32)
        onesbf = sb("onesbf", [128, 128], BF16)
        epsc = sb("epsc", [128, 1], F32)
        eps128 = sb("eps128", [128, 1], F32)
        zeroc = sb("zeroc", [128, 1], F32)
        sq = [sb("sq%d" % i, [128, NT], F32) for i in range(2)]
        rstd = [sb("rstd%d" % i, [128, NT], F32) for i in range(2)]
        lnt = [sb("lnt%d" % i, [128, NT], F32) for i in range(1)]
        sg = [sb("sg%d" % i, [128, NT], BF16) for i in range(2)]
        ps = [stack.enter_context(nc.psum_tensor("ps%d" % i, [128, NT], F32)) for i in range(8)]
        vgain = sb("vgain", [128, D], F32)
        bsb = sb("bsb", [128, 16, 128], F32)
        wsTm = sb("wsTm", [128, 16, 128], BF16)
        cmask = sb("cmask", [128, 128], F32)
        ssqv = sb("ssqv", [128, 4], F32)
        rsv = sb("rsv", [128, 4], F32)
        lnv = sb("lnv", [128, 4], F32)
        KW = 640 + 1024 + 2560
        kbuf = [sb("kbuf%d" % i, [128, KW], BF16) for i in range(2)]
        vbuf = [sb("vbuf%d" % i, [128, 33, 128], BF16) for i in range(2)]
        dtl = sb("dtl", [128, 9, 128], F32)
        nhm = sb("nhm", [128, 1], F32)
        tsc = [sb("tsc%d" % i, [128, NT], F32) for i in range(2)]
        pT = [sb("pT%d" % i, [128, NT], BF16) for i in range(3)]
        tadd = tsc
        kst = pT
        rl = sq

        def emit_all(S, W):
            bank = Ring(range(8))
            sqr = Ring(range(2))
            rsr = Ring(range(2))
            lnr = Ring(range(1))
            sgr = Ring(range(2))

            S.op("sp", lambda e: e.dma_start(out=gains[:], in_=T["gains"]), writes=["gains"], dma="c0")
            S.op("dve", lambda e: e.memset(ones32[:], 1.0), writes=["ones32"])
            S.op("dve", lambda e: e.memset(onesbf[:], 1.0), writes=["onesbf"])
            S.op("dve", lambda e: e.memset(epsc[:], EPS), writes=["epsc"])
            S.op("dve", lambda e: e.memset(eps128[:], EPS * 128.0), writes=["eps128"])
            S.op("dve", lambda e: e.memset(zeroc[:], 0.0), writes=["zeroc"])
            S.op("sp", lambda e: e.dma_start(out=cmask[:], in_=T["cmask"]), writes=["cmask"], dma="c1")
            S.op("sp", lambda e: e.dma_start(out=dtl[:], in_=T["dtiles"].rearrange("n p q -> p n q")),
                 writes=["dtl"], dma="c2")
            S.op("sp", lambda e: e.dma_start(out=nhm[:], in_=T["nhm"]), writes=["nhm"], dma="c3")

            XR = ["x:%d" % c for c in range(KC)]
            HR = ["h:%d" % c for c in range(KC)]

            def rmsnorm(gcol):
                b = bank.next()
                for c in range(KC):
                    si = sgr.next()
                    S.op("act", lambda e, c=c, si=si: e.activation(out=sg[si][:], in_=x[:, c, :], func=AF.Square),
                         reads=["x:%d" % c], writes=["sg:%d" % si])
                    S.op("pe", lambda e, c=c, si=si, b=b: e.matmul(ps[b][:], onesbf[:], sg[si][:],
                                                                    start=(c == 0), stop=(c == KC - 1)),
                         reads=["sg:%d" % si, "onesbf"], writes=["ps:%d" % b])
                ri = rsr.next()
                S.op("act", lambda e, b=b, ri=ri: e.activation(out=lnt[0][:], in_=ps[b][:], func=AF.Ln,
                                                                bias=epsc[:, 0:1], scale=1.0 / D),
                     reads=["ps:%d" % b, "epsc"], writes=["lnt:0"])
                S.op("act", lambda e, ri=ri: e.activation(out=rstd[ri][:], in_=lnt[0][:], func=AF.Exp, scale=-0.5),
                     reads=["lnt:0"], writes=["rstd:%d" % ri])
                for c in range(KC):
                    S.op("dve", lambda e, c=c, ri=ri: e.scalar_tensor_tensor(
                        out=h[:, c, :], in0=x[:, c, :], scalar=gains[:, gcol + c:gcol + c + 1], in1=rstd[ri][:],
                        op0=ALU.mult, op1=ALU.mult),
                        reads=["x:%d" % c, "rstd:%d" % ri, "gains"], writes=["h:%d" % c])

            def ffn(nm, l, gcol, after_x=None):
                rmsnorm(gcol)
                wg, wu, wd = T[nm + "_w_gate"][l], T[nm + "_w_up"][l], T[nm + "_w_down"][l]
                for fj in range(FC // 2):
                    pg, sgi = W.next(wpiece(wg, 0, KC, fj * PC), KC)
                    pu, sui = W.next(wpiece(wu, 0, KC, fj * PC), KC)
                    grp = []
                    for cc in range(2):
                        grp.append((pg, sgi, cc, bank.next()))
                        grp.append((pu, sui, cc, bank.next()))
                    if fj == 0:
                        for k in range(KC):
                            for (pw, swi, cc, b) in grp:
                                S.op("pe", lambda e, k=k, cc=cc, pw=pw, b=b: e.matmul(
                                    ps[b][:], pw[:, k, cc * 128:(cc + 1) * 128], h[:, k, :], start=(k == 0), stop=(k == KC - 1)),
                                    reads=["w:%d" % swi, "h:%d" % k], writes=["ps:%d" % b])
                    else:
                        for (pw, swi, cc, b) in grp:
                            for k in range(KC):
                                S.op("pe", lambda e, k=k, cc=cc, pw=pw, b=b: e.matmul(
                                    ps[b][:], pw[:, k, cc * 128:(cc + 1) * 128], h[:, k, :], start=(k == 0), stop=(k == KC - 1)),
                                    reads=["w:%d" % swi, "h:%d" % k], writes=["ps:%d" % b])
                    for cc in range(2):
                        f = 2 * fj + cc
                        bg, bu = grp[2 * cc][3], grp[2 * cc + 1][3]
                        gi = sgr.next()
                        S.op("act", lambda e, bg=bg, gi=gi: e.activation(out=sg[gi][:], in_=ps[bg][:], func=AF.Silu),
                             reads=["ps:%d" % bg], writes=["sg:%d" % gi])
                        S.op("dve", lambda e, bu=bu, gi=gi, f=f: e.tensor_tensor(
                            out=big[:, f, :], in0=ps[bu][:], in1=sg[gi][:], op=ALU.mult),
                            reads=["ps:%d" % bu, "sg:%d" % gi], writes=["big:%d" % f])
                for cb in range(D // PC):
                    bb = [bank.next(), bank.next()]
                    for kg in range(3):
                        nk = KC if kg < 2 else FC - 2 * KC
                        pd, sdi = W.next(wpiece(wd, kg * KC, nk, cb * PC), nk)
                        for cc in range(2):
                            for k in range(nk):
                                kk = kg * KC + k
                                S.op("pe", lambda e, k=k, kk=kk, cc=cc, pd=pd, b=bb[cc]: e.matmul(
                                    ps[b][:], pd[:, k, cc * 128:(cc + 1) * 128], big[:, kk, :],
                                    start=(kk == 0), stop=(kk == FC - 1)),
                                    reads=["w:%d" % sdi, "big:%d" % kk], writes=["ps:%d" % bb[cc]])
                    for cc in range(2):
                        dch = 2 * cb + cc
                        S.op("dve", lambda e, dch=dch, b=bb[cc]: e.scalar_tensor_tensor(
                            out=x[:, dch, :], in0=ps[b][:], scalar=0.5, in1=x[:, dch, :], op0=ALU.mult, op1=ALU.add),
                            reads=["ps:%d" % bb[cc], "x:%d" % dch], writes=["x:%d" % dch])
                        if after_x is not None:
                            after_x(dch)

            def proj_residual(wmat, src_chunks, src_res):
                for cb in range(D // PC):
                    pw, si = W.next(wpiece(wmat, 0, KC, cb * PC), KC)
                    for cc in range(2):
                        b = bank.next()
                        dch = 2 * cb + cc
                        for k in range(KC):
                            S.op("pe", lambda e, k=k, cc=cc, pw=pw, b=b: e.matmul(
                                ps[b][:], pw[:, k, cc * 128:(cc + 1) * 128], src_chunks(k), start=(k == 0), stop=(k == KC - 1)),
                                reads=["w:%d" % si, src_res(k)], writes=["ps:%d" % b])
                        S.op("dve", lambda e, dch=dch, b=b: e.tensor_tensor(
                            out=x[:, dch, :], in0=ps[b][:], in1=x[:, dch, :], op=ALU.add),
                            reads=["ps:%d" % b, "x:%d" % dch], writes=["x:%d" % dch])

            def headnorm_proj(wmat, col0, nchunks, gain_col_of, out_writer, eps_tile, eps_res):
                pending = []

                def post(ch, b, si):
                    b2 = bank.next()
                    S.op("pe", lambda e, b2=b2, si=si: e.matmul(ps[b2][:], onesbf[:], sg[si][:], start=True, stop=True),
                         reads=["sg:%d" % si, "onesbf"], writes=["ps:%d" % b2])
                    ri = rsr.next()
                    S.op("act", lambda e, b2=b2, ri=ri: e.activation(out=lnt[0][:], in_=ps[b2][:], func=AF.Ln,
                                                                      bias=eps_tile[:, 0:1], scale=1.0 / 128.0 if eps_res == "epsc" else 1.0),
                         reads=["ps:%d" % b2, eps_res], writes=["lnt:0"])
                    S.op("act", lambda e, ri=ri: e.activation(out=rstd[ri][:], in_=lnt[0][:], func=AF.Exp, scale=-0.5),
                         reads=["lnt:0"], writes=["rstd:%d" % ri])
                    out_writer(ch, b, ri)

                pj = 0
                while pj < nchunks // 2:
                    npc = 2 if pj == 0 else 1
                    grp = []
                    for q_ in range(npc):
                        pw, si_w = W.next(wpiece(wmat, 0, KC, col0 + (pj + q_) * PC), KC)
                        for cc in range(2):
                            grp.append((pw, si_w, cc, bank.next(), 2 * (pj + q_) + cc))
                    if npc == 2:
                        order = [(k, gi) for k in range(KC) for gi in range(len(grp))]
                    else:
                        order = [(k, gi) for gi in range(len(grp)) for k in range(KC)]
                    done = set()
                    for (k, gi) in order:
                        pw, si_w, cc, b, ch = grp[gi]
                        S.op("pe", lambda e, k=k, cc=cc, pw=pw, b=b: e.matmul(
                            ps[b][:], pw[:, k, cc * 128:(cc + 1) * 128], h[:, k, :], start=(k == 0), stop=(k == KC - 1)),
                            reads=["w:%d" % si_w, "h:%d" % k], writes=["ps:%d" % b])
                        if npc == 1 and k == KC - 1:
                            si = sgr.next()
                            S.op("act", lambda e, b=b, si=si: e.activation(out=sg[si][:], in_=ps[b][:], func=AF.Square),
                                 reads=["ps:%d" % b], writes=["sg:%d" % si])
                            if pending:
                                post(*pending.pop(0))
                            pending.append((ch, b, si))
                    if npc == 2:
                        for (pw, si_w, cc, b, ch) in grp:
                            si = sgr.next()
                            S.op("act", lambda e, b=b, si=si: e.activation(out=sg[si][:], in_=ps[b][:], func=AF.Square),
                                 reads=["ps:%d" % b], writes=["sg:%d" % si])
                            post(ch, b, si)
                    pj += npc
                while pending:
                    post(*pending.pop(0))

            def perm_view(ap, g):
                if g == 0:
                    return ap
                return ap.rearrange("p (j r) -> p r j", r=PERM_R[g])

            def perm_dst(ap, g):
                if g == 0:
                    return ap
                return ap.rearrange("p (r j) -> p r j", r=PERM_R[g])

            def gmlp(l, gcol):
                rmsnorm(gcol)
                S.op("sp", lambda e: e.dma_start(out=vgain[:], in_=T["vgain"][l].partition_broadcast(128)),
                     writes=["vgain"], dma="g0")
                S.op("sp", lambda e: e.dma_start(out=bsb[:].rearrange("p g q -> p (g q)"),
                                                 in_=T["bs"][l].partition_broadcast(128)),
                     writes=["bsb"], dma="g1")
                S.op("pool", lambda e: e.dma_start(out=wsTm[:].rearrange("p g q -> p (g q)"), in_=T["wsT"][l]),
                     writes=["wsTm"], dma="g2")
                S.op("dve", lambda e: e.tensor_tensor(out=wsTm[:], in0=wsTm[:],
                                                      in1=cmask[:].unsqueeze(1).broadcast_to([128, 16, 128]), op=ALU.mult),
                     reads=["wsTm", "cmask"], writes=["wsTm"])
                win = T["gmlp_w_in"][l]
                gv = big[:, 16:32, :].rearrange("p c t -> p (c t)").rearrange("p (b f) -> p b f", b=4)
                for pj in range(8):
                    pw, si = W.next(wpiece(win, 0, KC, D + pj * PC), KC)
                    bks = [bank.next() for _ in range(4)]
                    if pj == 0:
                        order = [(k, blk) for k in range(KC) for blk in range(4)]
                    else:
                        order = [(k, blk) for blk in range(4) for k in range(KC)]
                    for (k, blk) in order:
                        b = bks[blk]
                        S.op("pe", lambda e, k=k, blk=blk, pw=pw, b=b: e.matmul(
                            ps[b][:, 0:PC], h[:, k, blk * 128:(blk + 1) * 128], pw[:, k, :], start=(k == 0), stop=(k == KC - 1)),
                            reads=["w:%d" % si, "h:%d" % k], writes=["ps:%d" % b])
                    for blk in range(4):
                        b = bks[blk]
                        ch = 16 + 4 * blk + pj // 2
                        S.op("act", lambda e, b=b, blk=blk, pj=pj: e.activation(
                            out=gv[:, blk, pj * PC:(pj + 1) * PC], in_=ps[b][:, 0:PC], func=AF.Gelu_apprx_tanh),
                            reads=["ps:%d" % b], writes=["big:%d" % ch])
                S.op("dve", lambda e: e.memset(ssqv[:], 0.0), writes=["ssqv"])
                junk = big[:, 32:36, :].rearrange("p c t -> p (c t)")
                for blk in range(4):
                    S.op("act", lambda e, blk=blk: e.activation(out=junk, in_=gv[:, blk, :], func=AF.Square,
                                                                accum_out=ssqv[:, blk:blk + 1]),
                         reads=["big:%d" % (16 + 4 * blk + i) for i in range(4)] + [],
                         writes=["big:32", "big:33", "big:34", "big:35", "ssqv"])
                S.op("act", lambda e: e.activation(out=lnv[:], in_=ssqv[:], func=AF.Ln, bias=epsc[:, 0:1], scale=1.0 / D),
                     reads=["ssqv", "epsc"], writes=["lnv"])
                S.op("act", lambda e: e.activation(out=rsv[:], in_=lnv[:], func=AF.Exp, scale=-0.5),
                     reads=["lnv"], writes=["rsv"])
                for blk in range(4):
                    rr = ["big:%d" % (16 + 4 * blk + i) for i in range(4)]
                    S.op("dve", lambda e, blk=blk: e.scalar_tensor_tensor(
                        out=gv[:, blk, :], in0=gv[:, blk, :], scalar=rsv[:, blk:blk + 1], in1=vgain[:],
                        op0=ALU.mult, op1=ALU.mult),
                        reads=rr + ["rsv", "vgain"], writes=rr)
                pj = 0
                while pj < 8:
                    npc = 1
                    grp = []
                    for q_ in range(npc):
                        pw, si = W.next(wpiece(win, 0, KC, (pj + q_) * PC), KC)
                        for cc in range(2):
                            grp.append((pw, si, cc, bank.next(), 2 * (pj + q_) + cc))
                    if npc == 2:
                        order = [(k, gi) for k in range(KC) for gi in range(len(grp))]
                    else:
                        order = [(k, gi) for gi in range(len(grp)) for k in range(KC)]
                    for (k, gi) in order:
                        pw, si, cc, b, f = grp[gi]
                        S.op("pe", lambda e, k=k, cc=cc, pw=pw, b=b: e.matmul(
                            ps[b][:], pw[:, k, cc * 128:(cc + 1) * 128], h[:, k, :], start=(k == 0), stop=(k == KC - 1)),
                            reads=["w:%d" % si, "h:%d" % k], writes=["ps:%d" % b])
                    for (pw, si, cc, b, f) in grp:
                        S.op("act", lambda e, b=b, f=f: e.activation(out=big[:, f, :], in_=ps[b][:], func=AF.Gelu_apprx_tanh),
                             reads=["ps:%d" % b], writes=["big:%d" % f])
                    pj += npc
                tr = Ring(range(2))
                for g in range(16):
                    b = bank.next()
                    for blk in range(4):
                        S.op("pe", lambda e, g=g, blk=blk, b=b: e.matmul(
                            ps[b][:, blk * 128:(blk + 1) * 128], gv[:, blk, g * 128:(g + 1) * 128], wsTm[:, g, :],
                            start=True, stop=True),
                            reads=["big:%d" % (16 + 4 * blk + g // 4), "wsTm"], writes=["ps:%d" % b])
                    ti = tr.next()
                    S.op("dve", lambda e, g=g, b=b, ti=ti: e.tensor_tensor(
                        out=tadd[ti][:].rearrange("p (b q) -> p b q", b=4),
                        in0=ps[b][:].rearrange("p (b q) -> p b q", b=4),
                        in1=bsb[:, g, :].unsqueeze(1).broadcast_to([128, 4, 128]), op=ALU.add),
                        reads=["ps:%d" % b, "bsb"], writes=["tsc:%d" % ti])
                    S.op("dve", lambda e, g=g, ti=ti: e.tensor_tensor(
                        out=big[:, 32 + g, :], in0=tadd[ti][:], in1=big[:, g, :], op=ALU.mult),
                        reads=["tsc:%d" % ti, "big:%d" % g], writes=["big:%d" % (32 + g)])
                proj_residual(T["gmlp_w_out"][l], lambda k: big[:, 32 + k, :], lambda k: "big:%d" % (32 + k))

            def kvproj(t, after_norm=None):
                rmsnorm(192)
                if after_norm is not None:
                    after_norm()
                wkv = T["w_kv"]
                kr = Ring(range(3))

                def kwriter(ch, b, ri):
                    g = ch // 16
                    ki = kr.next()
                    S.op("dve", lambda e, b=b, ri=ri, g=g, ki=ki: e.scalar_tensor_tensor(
                        out=perm_dst(kst[ki][:], g), in0=perm_view(ps[b][:], g), scalar=gains[:, 208 + g:209 + g],
                        in1=perm_view(rstd[ri][:], g), op0=ALU.mult, op1=ALU.mult),
                        reads=["ps:%d" % b, "rstd:%d" % ri, "gains"], writes=["pt:%d" % ki])
                    S.op("sp", lambda e, ch=ch, ki=ki: e.dma_start(out=T["KTt%dg%d" % (t, ch // 16)][ch % 16], in_=kst[ki][:]),
                         reads=["pt:%d" % ki], writes=["KTt:%d:%d" % (t, ch)], dma="k%d" % ki)

                headnorm_proj(wkv, 0, 48, None, kwriter, epsc, "epsc")
                for c in range(KC):
                    S.op("act", lambda e, c=c: e.activation(out=perm_dst(big[:, 16 + c, :], 1), in_=perm_view(h[:, c, :], 1),
                                                            func=AF.Copy),
                         reads=["h:%d" % c], writes=["big:%d" % (16 + c)])
                for g in range(3):
                    base = 0 if g % 2 == 0 else 32
                    stg = big[:, base:base + 16, :].rearrange("p c t -> p (c t)").rearrange("p (b f) -> p b f", b=4)
                    for pj in range(8):
                        pw, si = W.next(wpiece(wkv, 0, KC, 3 * D + g * D + pj * PC), KC)
                        for blk in range(4):
                            b = bank.next()
                            for k in range(KC):
                                if g == 0:
                                    lt = lambda k=k, blk=blk: h[:, k, blk * 128:(blk + 1) * 128]
                                    lr = "h:%d" % k
                                else:
                                    lt = lambda k=k, blk=blk: big[:, 16 + k, blk * 128:(blk + 1) * 128]
                                    lr = "big:%d" % (16 + k)
                                S.op("pe", lambda e, k=k, lt=lt, pw=pw, b=b: e.matmul(
                                    ps[b][:, 0:PC], lt(), pw[:, k, :], start=(k == 0), stop=(k == KC - 1)),
                                    reads=["w:%d" % si, lr], writes=["ps:%d" % b])
                            ch = base + 4 * blk + pj // 2
                            eng = "act" if (blk % 2 == 0) else "dve"
                            if eng == "act":
                                S.op("act", lambda e, b=b, blk=blk, pj=pj, stg=stg: e.activation(
                                    out=stg[:, blk, pj * PC:(pj + 1) * PC], in_=ps[b][:, 0:PC], func=AF.Copy),
                                    reads=["ps:%d" % b], writes=["big:%d" % ch])
                            else:
                                S.op("dve", lambda e, b=b, blk=blk, pj=pj, stg=stg: e.tensor_copy(
                                    out=stg[:, blk, pj * PC:(pj + 1) * PC], in_=ps[b][:, 0:PC]),
                                    reads=["ps:%d" % b], writes=["big:%d" % ch])
                    S.op("sp", lambda e, g=g, stg=stg: e.dma_start(
                        out=T["Vt%dg%d" % (t, g)].rearrange("(b p) f -> p b f", p=128), in_=stg),
                        reads=["big:%d" % (base + i) for i in range(16)], writes=["Vt:%d:%d" % (t, g)], dma="v%d" % g)
                for g in range(3):
                    if not (g == 2 or t == ntile - 1):
                        continue
                    S.op("pool", lambda e, g=g: e.collective_compute(
                        "AllGather", ALU.bypass, replica_groups=[[0, 1], [2, 3], [4, 5], [6, 7]],
                        ins=[T["KTt%dg%d" % (t, g)].opt()], outs=[T["KTg%dg%d" % (t, g)].opt()]),
                        reads=["KTt:%d:%d" % (t, g * 16 + c) for c in range(16)], writes=["KTg:%d:%d" % (t, g)],
                        dma="cck%d%d" % (t, g), inc=1)
                    S.op("pool", lambda e, g=g: e.collective_compute(
                        "AllGather", ALU.bypass, replica_groups=[[0, 1], [2, 3], [4, 5], [6, 7]],
                        ins=[T["Vt%dg%d" % (t, g)].opt()], outs=[T["Vg%dg%d" % (t, g)].opt()]),
                        reads=["Vt:%d:%d" % (t, g)], writes=["Vg:%d:%d" % (t, g)], dma="ccv%d%d" % (t, g), inc=1)

            def attention(j, gcol, t):
                rmsnorm(gcol)
                wq = T["attn_w_q"][j]

                def qwriter(ch, b, ri):
                    g = ch // 16
                    S.op("dve", lambda e, b=b, ri=ri, g=g, ch=ch: e.scalar_tensor_tensor(
                        out=perm_dst(big[:, ch, :], g), in0=perm_view(ps[b][:], g),
                        scalar=gains[:, 211 + 3 * j + g:212 + 3 * j + g],
                        in1=perm_view(rstd[ri][:], g), op0=ALU.mult, op1=ALU.mult),
                        reads=["ps:%d" % b, "rstd:%d" % ri, "gains"], writes=["big:%d" % ch])

                headnorm_proj(wq, 0, 48, None, qwriter, eps128, "eps128")

                def ksrc(ch, kt):
                    g, hh = ch // 16, ch % 16
                    if kt >= 0:
                        return T["KTt%dg%d" % (kt, g)][hh], ["KTt:%d:%d" % (kt, ch)]
                    return T["KTg%dg%d" % (NTILE + kt, g)][hh], ["KTg:%d:%d" % (NTILE + kt, g)]

                def vsrc(g, kt):
                    if kt >= 0:
                        return T["Vt%dg%d" % (kt, g)], ["Vt:%d:%d" % (kt, g)]
                    return T["Vg%dg%d" % (NTILE + kt, g)], ["Vg:%d:%d" % (NTILE + kt, g)]

                def load_kv(hd):
                    bi = hd % 2
                    kb, vb = kbuf[bi], vbuf[bi]
                    c0 = hd * 128
                    kl = [(0, 0, 0, 128, t - 1, 384), (0, 1, 128, 512, t, 0),
                          (1, 0, 640, 512, t - 1, 0), (1, 1, 1152, 512, t, 0)]
                    kl += [(2, i, 1664 + i * 512, 512, t - 4 + i, 0) for i in range(5)]
                    for (g, sub, d0, n, kt, s0) in kl:
                        src, rr = ksrc(g * 16 + hd, kt)
                        S.op("sp", lambda e, d0=d0, n=n, s0=s0, src=src: e.dma_start(out=kb[:, d0:d0 + n], in_=src[:, s0:s0 + n]),
                             reads=rr, writes=["kbuf:%d:%d:%d" % (bi, g, sub)], dma="kb%dg%di%d" % (bi, g, sub))
                    vl = [(0, 0, 0, 1, t - 1, 384), (0, 1, 1, 4, t, 0),
                          (1, 0, 5, 4, t - 1, 0), (1, 1, 9, 4, t, 0)]
                    vl += [(2, i, 13 + 4 * i, 4, t - 4 + i, 0) for i in range(5)]
                    for (g, sub, b0, nb, kt, r0) in vl:
                        src, rr = vsrc(g, kt)
                        S.op("sp", lambda e, b0=b0, nb=nb, r0=r0, src=src: e.dma_start(
                            out=vb[:, b0:b0 + nb, :],
                            in_=src[r0:r0 + nb * 128, c0:c0 + 128].rearrange("(b p) d -> p b d", p=128)),
                            reads=rr, writes=["vbuf:%d:%d:%d" % (bi, g, sub)], dma="vb%dg%di%d" % (bi, g, sub))

                def units(hd):
                    u = []
                    u.append((0, 128, 1, 0, [False] * 4, [1]))
                    u.append((0, 0, 0, 1, [t == 0, False, False, False], [0, 1]))
                    u.append((1, 640 + 512, 5 + 4, 2, [False] * 4, [1]))
                    u.append((1, 640, 5, 3, [t == 0] * 4, [0]))
                    for dT in range(5):
                        halo = (t - dT) < 0
                        u.append((2, 1664 + (4 - dT) * 512, 13 + (4 - dT) * 4, 4 + dT, [halo] * 4, [4 - dT]))
                    return u

                sbank = Ring([0, 1, 2, 3])
                tr = Ring(range(2))
                pr = Ring(range(3))
                rlr = sqr
                load_kv(0)
                pend = []

                def pv(hd, g, vb0, pi, ob, lb, first, last, subs):
                    bi = hd % 2
                    vres = ["vbuf:%d:%d:%d" % (bi, g, sb_) for sb_ in subs]
                    for i in range(4):
                        if g == 0:
                            oa = lambda bk, i=i: ps[bk][:, i * 128:(i + 1) * 128]
                            ra = pT[pi][:, i * 128:(i + 1) * 128]
                        else:
                            oa = lambda bk, i=i: ps[bk][:].rearrange("p (j r) -> p r j", r=4)[:, i, :]
                            ra = pT[pi][:, i * 128:(i + 1) * 128]
                        S.op("pe", lambda e, oa=oa, ra=ra, i=i: e.matmul(
                            oa(ob), vbuf[bi][:, vb0 + i, :], ra, start=(first and i == 0), stop=last, skip_group_check=True),
                            reads=vres + ["pt:%d" % pi], writes=["ps:%d" % ob])
                        S.op("pe", lambda e, oa=oa, ra=ra, i=i: e.matmul(
                            oa(lb), onesbf[:], ra, start=(first and i == 0), stop=last, skip_group_check=True),
                            reads=["onesbf", "pt:%d" % pi], writes=["ps:%d" % lb])

                LAG = 2

                fin_q = []

                def finalize(hd_, ob_, lb_):
                    ri = rlr.next()
                    S.op("act", lambda e, lb=lb_: e.activation(out=lnt[0][:], in_=ps[lb][:], func=AF.Ln),
                         reads=["ps:%d" % lb_], writes=["lnt:0"])
                    S.op("act", lambda e, ri=ri: e.activation(out=rl[ri][:], in_=lnt[0][:], func=AF.Exp, scale=-1.0),
                         reads=["lnt:0"], writes=["sq:%d" % ri])
                    S.op("dve", lambda e, ob=ob_, ri=ri, hd=hd_: e.tensor_tensor(
                        out=h[:, hd, :], in0=ps[ob][:], in1=rl[ri][:], op=ALU.mult),
                        reads=["ps:%d" % ob_, "sq:%d" % ri], writes=["h:%d" % hd_])

                def flush_one():
                    tup = pend.pop(0)
                    pv(*tup)
                    if tup[7]:
                        fin_q.append([3, tup[0], tup[4], tup[5]])

                def tick_fin(force=False):
                    for f in list(fin_q):
                        f[0] -= 1
                        if f[0] <= 0 or force:
                            fin_q.remove(f)
                            finalize(f[1], f[2], f[3])

                for hd in range(NHEAD):
                    bi = hd % 2
                    ob, lb = (4, 5) if hd % 2 == 0 else (6, 7)
                    slope = float(2.0 ** (-8.0 * (hd + 1) / NHEAD))
                    us = units(hd)
                    for ui, (g, kc0, vb0, di, halo, subs) in enumerate(us):
                        sbk = sbank.next()
                        qch = g * 16 + hd
                        for i in range(4):
                            S.op("pe", lambda e, i=i, kc0=kc0, sbk=sbk, qch=qch, bi=bi: e.matmul(
                                ps[sbk][:, i * 128:(i + 1) * 128], kbuf[bi][:, kc0 + i * 128:kc0 + (i + 1) * 128],
                                big[:, qch, i * 128:(i + 1) * 128], start=True, stop=True),
                                reads=["kbuf:%d:%d:%d" % (bi, g, sb_) for sb_ in subs] + ["big:%d" % qch], writes=["ps:%d" % sbk])
                        ti = tr.next()
                        S.op("dve", lambda e, sbk=sbk, ti=ti, di=di, slope=slope: e.scalar_tensor_tensor(
                            out=tsc[ti][:].rearrange("p (b q) -> p b q", b=4),
                            in0=dtl[:, di, :].unsqueeze(1).broadcast_to([128, 4, 128]), scalar=-slope,
                            in1=ps[sbk][:].rearrange("p (b q) -> p b q", b=4), op0=ALU.mult, op1=ALU.add),
                            reads=["ps:%d" % sbk, "dtl"], writes=["tsc:%d" % ti])
                        pi = pr.next()
                        segs = []
                        i0 = 0
                        while i0 < 4:
                            i1 = i0
                            while i1 < 4 and halo[i1] == halo[i0]:
                                i1 += 1
                            segs.append((i0, i1, halo[i0]))
                            i0 = i1
                        for (a0, a1, hl) in segs:
                            bt, br = (nhm, "nhm") if hl else (zeroc, "zeroc")
                            S.op("act", lambda e, a0=a0, a1=a1, bt=bt, ti=ti, pi=pi: e.activation(
                                out=pT[pi][:, a0 * 128:a1 * 128], in_=tsc[ti][:, a0 * 128:a1 * 128], func=AF.Exp,
                                bias=bt[:, 0:1], scale=1.0),
                                reads=["tsc:%d" % ti, br], writes=["pt:%d" % pi])
                        pend.append((hd, g, vb0, pi, ob, lb, ui == 0, ui == len(us) - 1, subs))
                        tick_fin()
                        if len(pend) > LAG:
                            flush_one()
                        if ui == LAG and hd + 1 < NHEAD:
                            load_kv(hd + 1)
                while pend:
                    flush_one()
                tick_fin(force=True)
                proj_residual(T["attn_w_o"][j], lambda k: h[:, k, :], lambda k: "h:%d" % k)

            xin_v = T["xin"].rearrange("(c p) t -> p c t", p=128)
            xmid_v = T["xmid"].rearrange("(c p) t -> p c t", p=128)
            xo_v = T["xout"].rearrange("(c p) t -> p c t", p=128)
            def store_chunk(dst_v, nm, t, c):
                S.op("sp", lambda e: e.dma_start(out=dst_v[:, c, t * NT:(t + 1) * NT], in_=x[:, c, :]),
                     reads=["x:%d" % c], writes=["%s:%d:%d" % (nm, t, c)], dma="xs%d" % (c % 8))

            def load_chunk(src_v, nm, t, c):
                rr = ["%s:%d:%d" % (nm, t, c)] if nm else []
                S.op("sp", lambda e: e.dma_start(out=x[:, c, :], in_=src_v[:, c, t * NT:(t + 1) * NT]),
                     reads=rr, writes=["x:%d" % c], dma="xl%d" % (c % 8))

            for c in range(KC):
                load_chunk(xin_v, None, 0, c)
            for t in range(ntile):
                W.begin_tile("A", t)
                for l in range(2):
                    ffn("ffn1", l, l * 48)
                    gmlp(l, l * 48 + 16)
                    ffn("ffn2", l, l * 48 + 32,
                        after_x=(lambda c, t=t: store_chunk(xmid_v, "xmid", t, c)) if l == 1 else None)

                def nxt(t=t):
                    for c in range(KC):
                        if t + 1 < ntile:
                            load_chunk(xin_v, None, t + 1, c)
                        else:
                            load_chunk(xmid_v, "xmid", 0, c)
                kvproj(t, after_norm=nxt)
            for t in range(ntile):
                W.begin_tile("B", t)
                for j in range(2):
                    gl = 2 + j

                    def fin(c, t=t):
                        store_chunk(xo_v, "xo", t, c)
                        if t + 1 < ntile:
                            load_chunk(xmid_v, "xmid", t + 1, c)
                    ffn("ffn1", gl, gl * 48)
                    attention(j, gl * 48 + 16, t)
                    ffn("ffn2", gl, gl * 48 + 32, after_x=fin if j == 1 else None)

        S0 = Sched(null=True)
        W0 = WStream(S0, wsl)
        emit_all(S0, W0)
        S1 = Sched()
        WBN = 192
        wbf = {}
        for ph, n in W0.count.items():
            tens = [nc.dram_tensor("wbf_%s%d" % (ph, q), [min(WBN, n - q * WBN), 128, KC * PC], BF16).ap()
                    for q in range((n + WBN - 1) // WBN)]
            wbf[ph] = [tens[pj // WBN][pj % WBN] for pj in range(n)]
        W1 = WStream(S1, wsl, W0.pieces, wbf)
        emit_all(S1, W1)
        assert W1.i == len(W0.pieces)
        S1.emit(nc, stack)
    return nc


def _gains(inp):
    g = np.zeros((128, NGCOL), np.float32)

    def fm(v):
        return np.ascontiguousarray(np.asarray(v, np.float32).reshape(16, 128).T)

    for l in range(4):
        g[:, l * 48:l * 48 + 16] = fm(inp["ffn1_norm"][l])
        g[:, l * 48 + 16:l * 48 + 32] = fm(inp["mix_norm"][l])
        g[:, l * 48 + 32:l * 48 + 48] = fm(inp["ffn2_norm"][l])
    g[:, 192:208] = fm(inp["kv_norm"])
    for gg in range(3):
        g[:, 208 + gg] = inp["k_norm"][gg]
    for j in range(2):
        for gg in range(3):
            g[:, 211 + 3 * j + gg] = inp["attn_q_norm"][j, gg]
    return g


def _dtiles():
    d = np.full((9, 128, 128), BIGD, np.float32)
    k = np.arange(128)[:, None]
    q = np.arange(128)[None, :]
    for gi, dil in ((0, 1), (1, 4)):
        cur = (q - k).astype(np.float32) * dil
        d[2 * gi] = np.where(q - k >= 0, cur, BIGD)
        prv = (128 + q - k).astype(np.float32) * dil
        d[2 * gi + 1] = np.where(k >= q, prv, BIGD)
    rk, jk = k % 4, k // 4
    rq, jq = q % 4, q // 4
    same = rk == rq
    for dT in range(5):
        delta = 32 * dT + jq - jk
        ok = same & (delta >= 0) & (delta <= 128)
        d[4 + dT] = np.where(ok, delta.astype(np.float32) * 16.0, BIGD)
    return d


_NC_CACHE = {}


def _get_nc(ntile=NTILE):
    if ntile not in _NC_CACHE:
        _NC_CACHE[ntile] = build(ntile)
    return _NC_CACHE[ntile]


def kernel(**inp):
    inp = {k: np.asarray(v) for k, v in inp.items()}
    x = inp["x"].astype(np.float32, copy=False)
    ncore = 8
    nc = _get_nc()
    w = {}
    for nm in ("ffn1", "ffn2"):
        for s in ("_w_gate", "_w_up", "_w_down"):
            w[nm + s] = inp[nm + s]
    w["gmlp_w_in"] = inp["gmlp_w_in"]
    w["gmlp_w_out"] = inp["gmlp_w_out"]
    w["w_kv"] = inp["w_kv"]
    w["wsT"] = np.ascontiguousarray(inp["gmlp_w_s"].transpose(0, 3, 1, 2).reshape(2, 128, 2048))
    w["vgain"] = inp["gmlp_v_norm"]
    w["bs"] = np.ascontiguousarray(inp["gmlp_b_s"].reshape(2, 2048))
    w["cmask"] = np.triu(np.ones((128, 128), np.float32))
    w["attn_w_q"] = inp["attn_w_q"]
    w["attn_w_o"] = inp["attn_w_o"]
    w["gains"] = _gains(inp)
    w["dtiles"] = _dtiles()
    in_maps = []
    for c in range(ncore):
        m = dict(w)
        m["xin"] = np.ascontiguousarray(x[c // 2, (c % 2) * TOK:(c % 2 + 1) * TOK, :].T)
        m["nhm"] = np.full((128, 1), NEGH if c % 2 == 0 else 0.0, np.float32)
        in_maps.append(m)
    res = run_bass_kernel_spmd(nc, in_maps, core_ids=list(range(ncore))).results
    out = np.empty((4, 4096, D), np.float32)
    for c in range(ncore):
        out[c // 2, (c % 2) * TOK:(c % 2 + 1) * TOK, :] = np.asarray(res[c]["xout"]).T
    return out
```
